# Optimizing a Trainium2 kernel written in Bass

```python
import math
import jax, jax.numpy as jnp
from jax import lax
import numpy as np

D_MODEL = 2048
BATCH = 4
SEQ = 8192
DEPTH = 1

D_MIX = D_MODEL
ATT_WIDTH = D_MIX // 2
ATT_HEADS = 16
ATT_HEAD_DIM = ATT_WIDTH // ATT_HEADS
DILATED_PATTERNS = ((128, 1), (512, 4), (2048, 16))
GLA_WIDTH = D_MIX - ATT_WIDTH
GLA_HEADS = 4
GLA_KEY_WIDTH = GLA_WIDTH // 2
GLA_DK = GLA_KEY_WIDTH // GLA_HEADS
GLA_DV = GLA_WIDTH // GLA_HEADS
GLA_GATE_RANK = 16
GLA_GATE_NORM = 16.0
GLA_CHUNK = 64
REL_BUCKETS = 32
REL_MAX_DIST = 1024
EPS = 1e-6
NEG_INF = -1e30

PROJ_SIZES = (ATT_WIDTH, ATT_WIDTH, ATT_WIDTH, ATT_WIDTH,
              GLA_KEY_WIDTH, GLA_KEY_WIDTH, GLA_WIDTH, GLA_WIDTH,
              GLA_GATE_RANK, GLA_GATE_RANK)
PROJ_COLS = int(sum(PROJ_SIZES))
PROJ_SPLITS = [int(s) for s in np.cumsum(PROJ_SIZES)[:-1]]

kernel_name = "hybrid_dilated_attn_gla_block"


def rms_norm(x):
    xf = x.astype(jnp.float32)
    return (xf * lax.rsqrt(jnp.mean(xf * xf, axis=-1, keepdims=True) + EPS)).astype(x.dtype)


def t5_bucket_np(rel):
    nb = REL_BUCKETS // 2
    max_exact = nb // 2
    n = np.abs(rel)
    large = max_exact + (np.log(np.maximum(n, 1) / max_exact)
                         / np.log(REL_MAX_DIST / max_exact) * (nb - max_exact)).astype(np.int32)
    large = np.minimum(large, nb - 1)
    return (np.where(rel > 0, nb, 0) + np.where(n < max_exact, n, large)).astype(np.int32)


def dilated_window_attention(q, k, v, rel_bias, window, dilation):
    B, S, H, E = q.shape
    w = window // (2 * dilation)
    L = S // dilation
    nb = -(-L // w)
    Lp = nb * w

    def residue_layout(t, pad_lo, pad_hi):
        t = t.reshape(B, L, dilation, H, E)
        return jnp.pad(t, ((0, 0), (pad_lo, pad_hi), (0, 0), (0, 0), (0, 0)))

    qb = residue_layout(q, 0, Lp - L).reshape(B, nb, w, dilation, H, E)

    def key_windows(t):
        tb = residue_layout(t, w, Lp - L + w).reshape(B, nb + 2, w, dilation, H, E)
        return jnp.concatenate([tb[:, :-2], tb[:, 1:-1], tb[:, 2:]], axis=2)

    kw = key_windows(k)
    vw = key_windows(v)
    s = jnp.einsum('bnqrhe,bnkrhe->bhrnqk', qb, kw).astype(jnp.float32) * (E ** -0.5)

    qi = np.arange(w)[:, None]
    kj = np.arange(3 * w)[None, :]
    step = kj - w - qi
    in_window = np.abs(step) <= w
    bucket = t5_bucket_np(step * dilation)
    kpos = np.arange(nb)[:, None] * w + np.arange(3 * w)[None, :] - w
    key_ok = (kpos >= 0) & (kpos < L)
    mask = in_window[None] & key_ok[:, None, :]

    bias = jnp.transpose(rel_bias[bucket], (2, 0, 1)).astype(jnp.float32)
    s = s + bias[None, :, None, None]
    s = jnp.where(mask, s, NEG_INF)
    m = jnp.max(s, axis=-1, keepdims=True)
    p = jnp.exp(s - m)
    den = jnp.sum(p, axis=-1, keepdims=True)
    o = jnp.einsum('bhrnqk,bnkrhe->bhrnqe', (p / den).astype(v.dtype), vw)
    lse = (m + jnp.log(den))[..., 0]
    o = o.transpose(0, 3, 4, 2, 1, 5).reshape(B, Lp, dilation, H, E)[:, :L].reshape(B, S, H, E)
    lse = lse.transpose(0, 3, 4, 2, 1).reshape(B, Lp, dilation, H)[:, :L].reshape(B, S, H)
    return o, lse


def gla_chunked(q, k, v, log_g):
    B, S, H, dk = q.shape
    dv = v.shape[-1]
    C = GLA_CHUNK
    N = S // C

    def chunks(t):
        return t.reshape(B, N, C, H, t.shape[-1]).transpose(0, 3, 1, 2, 4)

    q, k, v, log_g = chunks(q), chunks(k), chunks(v), chunks(log_g)
    b = jnp.cumsum(log_g.astype(jnp.float32), axis=3)
    b_last = b[:, :, :, -1:]
    qf = q.astype(jnp.float32) * jnp.exp(b) * (dk ** -0.5)
    kf = k.astype(jnp.float32)
    vf = v.astype(jnp.float32)
    causal = np.tril(np.ones((C, C), dtype=bool))
    att = jnp.einsum('bhnid,bhnjd->bhnij', qf, kf * jnp.exp(-b))
    att = jnp.where(causal, att, 0.0)
    o_intra = jnp.einsum('bhnij,bhnjv->bhniv', att, vf)
    kv = jnp.einsum('bhnjd,bhnjv->bhndv', kf * jnp.exp(b_last - b), vf)
    decay = jnp.exp(b_last[:, :, :, 0])

    def step(state, inp):
        kv_n, g_n = inp
        return g_n[..., None] * state + kv_n, state

    init = jnp.zeros((B, H, dk, dv), jnp.float32)
    _, states = lax.scan(step, init, (jnp.moveaxis(kv, 2, 0), jnp.moveaxis(decay, 2, 0)))
    states = jnp.moveaxis(states, 0, 2)
    o = o_intra + jnp.einsum('bhnid,bhndv->bhniv', qf, states)
    return o.transpose(0, 2, 3, 1, 4).reshape(B, S, H, dv).astype(v.dtype)


def setup_inputs(seed: int = 0) -> dict:
    key = jax.random.key(seed)
    ks = jax.random.split(key, 14)
    f32 = jnp.float32
    x = jax.random.normal(ks[0], (BATCH, SEQ, D_MODEL), f32)
    c = jax.random.normal(ks[1], (BATCH, D_MODEL), f32)
    w_cond = jax.random.normal(ks[2], (DEPTH, D_MODEL, 3 * D_MODEL), f32) * (0.5 * D_MODEL ** -0.5)
    b_cond = jax.random.normal(ks[3], (DEPTH, 3 * D_MODEL), f32) * 0.02
    w_in = jax.random.normal(ks[4], (DEPTH, D_MODEL, PROJ_COLS), f32) * (D_MODEL ** -0.5)
    gla_gate_up_fwd = jax.random.normal(ks[5], (DEPTH, GLA_GATE_RANK, GLA_KEY_WIDTH), f32) * (GLA_GATE_RANK ** -0.5)
    gla_gate_bias_fwd = jax.random.normal(ks[6], (DEPTH, GLA_KEY_WIDTH), f32) * 0.01
    gla_gate_up_bwd = jax.random.normal(ks[7], (DEPTH, GLA_GATE_RANK, GLA_KEY_WIDTH), f32) * (GLA_GATE_RANK ** -0.5)
    gla_gate_bias_bwd = jax.random.normal(ks[8], (DEPTH, GLA_KEY_WIDTH), f32) * 0.01
    gla_norm_gain = 1.0 + 0.02 * jax.random.normal(ks[9], (DEPTH, GLA_WIDTH), f32)
    rel_bias = jax.random.normal(ks[10], (REL_BUCKETS, ATT_HEADS), f32) * 0.5
    w_out = jax.random.normal(ks[11], (DEPTH, D_MIX, D_MODEL), f32) * (D_MIX ** -0.5)
    final_gain = 1.0 + 0.02 * jax.random.normal(ks[12], (D_MODEL,), f32)
    return {"x": x, "c": c, "w_cond": w_cond, "b_cond": b_cond, "w_in": w_in,
            "gla_gate_up_fwd": gla_gate_up_fwd, "gla_gate_bias_fwd": gla_gate_bias_fwd,
            "gla_gate_up_bwd": gla_gate_up_bwd, "gla_gate_bias_bwd": gla_gate_bias_bwd,
            "gla_norm_gain": gla_norm_gain, "rel_bias": rel_bias, "w_out": w_out,
            "final_gain": final_gain}


def reference(x, c, w_cond, b_cond, w_in, gla_gate_up_fwd, gla_gate_bias_fwd,
              gla_gate_up_bwd, gla_gate_bias_bwd, gla_norm_gain, rel_bias, w_out, final_gain):
    B, S, _ = x.shape
    for layer in range(DEPTH):
        mod = jax.nn.silu(c) @ w_cond[layer] + b_cond[layer]
        shift, scale, gate = jnp.split(mod, 3, axis=-1)
        h = rms_norm(x) * (1.0 + scale[:, None]) + shift[:, None]

        proj = h @ w_in[layer]
        aq, ak, av, ag, gq, gk, gv, gg, lr_f, lr_b = jnp.split(proj, PROJ_SPLITS, axis=-1)

        hs = (B, S, ATT_HEADS, ATT_HEAD_DIM)
        aq, ak, av = aq.reshape(hs), ak.reshape(hs), av.reshape(hs)
        outs, lses = [], []
        for window, dilation in DILATED_PATTERNS:
            o_p, lse_p = dilated_window_attention(aq, ak, av, rel_bias, window, dilation)
            outs.append(o_p)
            lses.append(lse_p)
        mix_w = jax.nn.softmax(jnp.stack(lses, axis=0), axis=0)
        att = jnp.einsum('pbsh,pbshe->bshe', mix_w.astype(av.dtype), jnp.stack(outs, axis=0))
        a_out = att.reshape(B, S, ATT_WIDTH) * jax.nn.silu(ag)

        ks_ = (B, S, GLA_HEADS, GLA_DK)
        gq, gk = gq.reshape(ks_), gk.reshape(ks_)
        gv = gv.reshape(B, S, GLA_HEADS, GLA_DV)
        log_g_f = (jax.nn.log_sigmoid((lr_f @ gla_gate_up_fwd[layer] + gla_gate_bias_fwd[layer]).astype(jnp.float32))
                   / GLA_GATE_NORM).reshape(ks_)
        log_g_b = (jax.nn.log_sigmoid((lr_b @ gla_gate_up_bwd[layer] + gla_gate_bias_bwd[layer]).astype(jnp.float32))
                   / GLA_GATE_NORM).reshape(ks_)
        o_fwd = gla_chunked(gq, gk, gv, log_g_f)
        o_bwd = jnp.flip(gla_chunked(jnp.flip(gq, 1), jnp.flip(gk, 1), jnp.flip(gv, 1),
                                     jnp.flip(log_g_b, 1)), 1)
        g_o = rms_norm(o_fwd + o_bwd).reshape(B, S, GLA_WIDTH) * gla_norm_gain[layer]
        g_out = g_o * jax.nn.silu(gg)

        y = jnp.concatenate([a_out, g_out], axis=-1) @ w_out[layer]
        x = x + gate[:, None] * y
    return rms_norm(x) * final_gain
```

```python
import contextlib
import numpy as np
import concourse.bass as bass
import concourse.mybir as mybir
from concourse.bass_utils import run_bass_kernel_spmd

F32 = mybir.dt.float32
BF16 = mybir.dt.bfloat16
AF = mybir.ActivationFunctionType
ALU = mybir.AluOpType

D = 2048
SEQ = 8192
TOWN = 4096
NCOL = 7200
EPS = 1e-6
DEBUG = {"stop": None, "export": False}


class Tok:
    __slots__ = ("sem", "key", "val")

    def __init__(self, sem, key, val):
        self.sem, self.key, self.val = sem, key, val


class Buf:
    __slots__ = ("w", "r")

    def __init__(self):
        self.w = None
        self.r = {}


class Eng:
    def __init__(self, name, eng, sem):
        self.name, self.eng, self.sem = name, eng, sem
        self.count = 0
        self.seen = {}


class Ctx:
    def __init__(self, nc, n_dma_sems=40):
        self.nc = nc
        self.E = {}
        for name, eng in (("pe", nc.tensor), ("act", nc.scalar), ("dve", nc.vector),
                          ("pool", nc.gpsimd), ("sp", nc.sync)):
            self.E[name] = Eng(name, eng, nc.alloc_semaphore("s_" + name))
        self.dsems = [[nc.alloc_semaphore("d%d" % i), "d%d" % i, 0] for i in range(n_dma_sems)]
        self.dnext = 0

    def _wait_tok(self, e, t):
        if t is None:
            return
        if e.name == "pe" and t.key == "s_pe":
            return
        if e.seen.get(t.key, 0) >= t.val:
            return
        e.eng.wait_ge(t.sem, t.val)
        e.seen[t.key] = t.val

    def _deps(self, e, reads, writes):
        for b in reads:
            self._wait_tok(e, b.w)
        for b in writes:
            self._wait_tok(e, b.w)
            for t in list(b.r.values()):
                self._wait_tok(e, t)

    def _reg(self, tok, reads, writes):
        for b in writes:
            b.w = tok
            b.r = {}
        for b in reads:
            b.r[tok.key] = tok

    def op(self, ename, fn, reads=(), writes=()):
        e = self.E[ename]
        self._deps(e, reads, writes)
        ins = fn(e.eng)
        e.count += 1
        ins.then_inc(e.sem, 1)
        tok = Tok(e.sem, "s_" + ename, e.count)
        self._reg(tok, reads, writes)
        return tok

    def group(self, ename, fns, reads=(), writes=()):
        e = self.E[ename]
        self._deps(e, reads, writes)
        ins = None
        for fn in fns:
            ins = fn(e.eng)
        e.count += 1
        ins.then_inc(e.sem, 1)
        tok = Tok(e.sem, "s_" + ename, e.count)
        self._reg(tok, reads, writes)
        return tok

    def dma(self, qname, out, in_, reads=(), writes=(), **kw):
        if DEBUG.get("nopooldma") and qname == "pool":
            qname = "sp"
        e = self.E[qname]
        self._deps(e, reads, writes)
        d = self.dsems[self.dnext]
        self.dnext = (self.dnext + 1) % len(self.dsems)
        if d[2] > 0 and e.seen.get(d[1], 0) < d[2]:
            e.eng.wait_ge(d[0], d[2])
            e.seen[d[1]] = d[2]
        e.eng.dma_start(out=out, in_=in_, **kw).then_inc(d[0], 16)
        d[2] += 16
        tok = Tok(d[0], d[1], d[2])
        self._reg(tok, reads, writes)
        return tok

    def barrier(self):
        for e in self.E.values():
            for o in self.E.values():
                if o is not e and o.count > 0:
                    self._wait_tok(e, Tok(o.sem, "s_" + o.name, o.count))
            for d in self.dsems:
                if d[2] > 0:
                    self._wait_tok(e, Tok(d[0], d[1], d[2]))

    def final_wait(self, ename="sp"):
        e = self.E[ename]
        for o in self.E.values():
            if o is not e and o.count > 0:
                self._wait_tok(e, Tok(o.sem, "s_" + o.name, o.count))
        for d in self.dsems:
            if d[2] > 0:
                self._wait_tok(e, Tok(d[0], d[1], d[2]))


def build_program():
    nc = bass.Bass("TRN2", target_bir_lowering=False)
    cx = Ctx(nc)

    def din(name, shape, dt=F32):
        return nc.dram_tensor(name, list(shape), dt, kind="ExternalInput").ap()

    scratch_kind = "ExternalOutput" if DEBUG["export"] else "Internal"

    def dscr(name, shape, dt=BF16):
        return nc.dram_tensor(name, list(shape), dt, kind=scratch_kind).ap()

    x = din("x", [SEQ, D])
    c_in = din("c", [D])
    w_cond = din("w_cond", [D, 3 * D])
    b_cond = din("b_cond", [1, 3 * D])
    w_in = din("w_in", [D, NCOL])
    up_f = din("up_f", [16, 512])
    up_b = din("up_b", [16, 512])
    gb_f = din("gb_f", [512])
    gb_b = din("gb_b", [512])
    gnorm = din("gnorm", [1, 1024])
    btab = din("btab", [48, 128, 256])
    w_out = din("w_out", [D, D])
    fgain = din("fgain", [1, D])
    ident_in = din("ident", [128, 128])
    maskf_in = din("maskf", [128, 128])
    maskb_in = din("maskb", [128, 128])
    amask_in = din("amask", [128, 512])
    negm_in = din("negm", [128, 512])
    out = nc.dram_tensor("out", [TOWN, D], F32, kind="ExternalOutput").ap()

    wbf = dscr("wbf", [D, NCOL])
    modrow = dscr("modrow", [1, 3 * D], F32)
    qT = dscr("qT", [1024, TOWN])
    kT = dscr("kT", [1024, TOWN + 1024])
    agT = dscr("agT", [1024, TOWN])
    vtm = dscr("vtm", [1024 + TOWN + 1024, 1024])
    gqT = dscr("gqT", [512, TOWN])
    gkT = dscr("gkT", [512, SEQ])
    gvs = dscr("gvs", [SEQ, 1024])
    ggs = dscr("ggs", [TOWN, 1024])
    lrT = dscr("lrT", [32, SEQ], F32)
    mixT = dscr("mixT", [D, TOWN])
    obs = dscr("obs", [TOWN, 1024], F32)

    ident = nc.alloc_sbuf_tensor("ident_bf", [128, 128], BF16)
    identf = nc.alloc_sbuf_tensor("identf", [128, 128], F32)
    sc1 = nc.alloc_sbuf_tensor("sc1", [128, 16], F32)
    shf = nc.alloc_sbuf_tensor("shf", [128, 16], F32)
    B_ident, B_mod = Buf(), Buf()

    def stop(tag):
        return DEBUG["stop"] == tag

    def finish():
        cx.final_wait("sp")
        return nc

    with contextlib.ExitStack() as es:
        def sb(shape, dt, name):
            return es.enter_context(nc.sbuf_tensor(name, list(shape), dt))

        def ps(shape, dt, name):
            return es.enter_context(nc.psum_tensor(name, list(shape), dt))

        cfm = sb([128, 16], F32, "p0_c")
        scl = sb([128, 16], F32, "p0_sc")
        wc = [sb([128, 4, 2048], F32, "p0_w%d" % i) for i in range(3)]
        mrow = sb([1, 3 * D], F32, "p0_mrow")
        brow = sb([1, 3 * D], F32, "p0_brow")
        pacc = [ps([1, 2048], F32, "p0_pa%d" % i) for i in range(2)]
        B_c, B_scl, B_mrow, B_brow = Buf(), Buf(), Buf(), Buf()
        B_wc = [Buf(), Buf(), Buf()]
        B_pacc = [Buf(), Buf()]

        cx.dma("sp", identf[:], ident_in[:, :], writes=[B_ident])
        cx.op("dve", lambda e: e.tensor_copy(out=ident[:], in_=identf[:]), reads=[B_ident], writes=[B_ident])
        cx.dma("sp", cfm[:], c_in.rearrange("(k p) -> p k", p=128), writes=[B_c], allow_slow_non_contiguous=True)
        cx.dma("sp", brow[:], b_cond[:, :], writes=[B_brow])
        cx.op("act", lambda e: e.activation(out=scl[:], in_=cfm[:], func=AF.Silu), reads=[B_c], writes=[B_scl])
        wcv = w_cond.rearrange("(k p) n -> p k n", p=128)
        it = 0
        for n3 in range(3):
            pa = pacc[n3 % 2]
            Bpa = B_pacc[n3 % 2]
            for k4 in range(4):
                wb = wc[it % 3]
                Bw = B_wc[it % 3]
                cx.dma(("sp", "pool", "act")[it % 3], wb[:], wcv[:, k4 * 4:(k4 + 1) * 4, n3 * 2048:(n3 + 1) * 2048],
                       writes=[Bw])
                fns = []
                for kk in range(4):
                    kc = k4 * 4 + kk
                    for nn in range(4):
                        fns.append(lambda e, kk=kk, nn=nn, kc=kc, wb=wb, pa=pa: e.matmul(
                            pa[:, nn * 512:(nn + 1) * 512], lhsT=scl[:, kc:kc + 1],
                            rhs=wb[:, kk, nn * 512:(nn + 1) * 512], start=(kc == 0), stop=(kc == 15)))
                cx.group("pe", fns, reads=[Bw, B_scl], writes=[Bpa])
                it += 1
            cx.op("dve", lambda e, n3=n3, pa=pa: e.tensor_tensor(out=mrow[:, n3 * 2048:(n3 + 1) * 2048], in0=pa[:, :],
                                                         in1=brow[:, n3 * 2048:(n3 + 1) * 2048], op=ALU.add),
                  reads=[Bpa, B_brow], writes=[B_mrow])
        B_modrow = Buf()
        cx.dma("sp", modrow[:, :], mrow[:], reads=[B_mrow], writes=[B_modrow])
        cx.dma("sp", shf[:], modrow[0, 0:D].rearrange("(k p) -> p k", p=128), reads=[B_modrow], writes=[B_mod],
               allow_slow_non_contiguous=True)
        cx.dma("sp", sc1[:], modrow[0, D:2 * D].rearrange("(k p) -> p k", p=128), reads=[B_modrow], writes=[B_mod],
               allow_slow_non_contiguous=True)
        cx.op("dve", lambda e: e.tensor_scalar(out=sc1[:], in0=sc1[:], scalar1=1.0, scalar2=None, op0=ALU.add),
              reads=[B_mod], writes=[B_mod])
        cx.barrier()
    if stop("p0"):
        return finish()

    OWN, HALO, FAR = 0, 1, 2
    GROUPS = [
        (0, 512, "fm", qT, 0, (OWN,)), (512, 512, "fm", qT, 512, (OWN,)),
        (1024, 512, "fm", kT, 0, (OWN, HALO)), (1536, 512, "fm", kT, 512, (OWN, HALO)),
        (2048, 512, "tm", vtm, 0, (OWN, HALO)), (2560, 512, "tm", vtm, 512, (OWN, HALO)),
        (3072, 512, "fm", agT, 0, (OWN,)), (3584, 512, "fm", agT, 512, (OWN,)),
        (4096, 512, "fm", gqT, 0, (OWN,)),
        (4608, 512, "fm", gkT, 0, (OWN, HALO, FAR)),
        (5120, 512, "tm", gvs, 0, (OWN, HALO, FAR)), (5632, 512, "tm", gvs, 512, (OWN, HALO, FAR)),
        (6144, 512, "tm", ggs, 0, (OWN,)), (6656, 512, "tm", ggs, 512, (OWN,)),
        (7168, 32, "lr", lrT, 0, (OWN, HALO, FAR)),
    ]
    with contextlib.ExitStack() as es:
        def sb(shape, dt, name):
            return es.enter_context(nc.sbuf_tensor(name, list(shape), dt))

        def ps(shape, dt, name):
            return es.enter_context(nc.psum_tensor(name, list(shape), dt))

        xf = [sb([128, D], F32, "a_xf%d" % i) for i in range(2)]
        junk = sb([128, D], BF16, "a_junk")
        xs = [sb([128, 4, D], BF16, "a_xs%d" % i) for i in range(2)]
        hT = sb([128, 16, 1024], BF16, "a_hT")
        wg = [sb([128, 16, 512], BF16, "a_wg%d" % i) for i in range(3)]
        wf32 = [sb([128, 16, 256], F32, "a_wf%d" % i) for i in range(2)]
        B_wf32 = [Buf(), Buf()]
        B_wbfg = [Buf() for _ in range(len(GROUPS))]
        w_inv = w_in.rearrange("(k p) c -> p k c", p=128)
        nwf = [0]
        stg = [sb([128, 4096], BF16, "a_stg%d" % i) for i in range(3)]
        lrstg = sb([32, 1024], F32, "a_lrstg")
        ssq = sb([128, 8], F32, "a_ssq")
        rst = sb([128, 8], F32, "a_rst")
        zero = sb([128, 1024], BF16, "a_zero")
        pp = [ps([128, 512], F32, "a_pp%d" % i) for i in range(4)]
        pt = [ps([128, 512], BF16, "a_pt%d" % i) for i in range(4)]
        B_xf = [Buf(), Buf()]
        B_junk = Buf()
        B_xs = [[Buf() for _ in range(4)] for _ in range(2)]
        B_hT = [[Buf(), Buf()] for _ in range(16)]
        B_wg = [Buf() for _ in range(3)]
        B_stg = [Buf() for _ in range(3)]
        B_lrstg = Buf()
        B_ss = [Buf() for _ in range(8)]
        B_pp = [Buf() for _ in range(4)]
        B_pt = [Buf() for _ in range(4)]
        B_zero = Buf()
        B_scr = Buf()

        cx.op("pool", lambda e: e.memset(zero[:], 0.0), writes=[B_zero])
        for i in range(8):
            cx.dma("pool", vtm[i * 128:(i + 1) * 128, :], zero[:], reads=[B_zero], writes=[B_scr])

        wbv = wbf.rearrange("(k p) c -> p k c", p=128)
        nx = 0
        npp = 0
        nwg = 0
        nstg = 0
        nev = 0
        nxc = [0]

        def xprep_sub(ti, g, s):
            tk = ti * 1024 + g * 512 + s * 128
            xb = xf[nxc[0] % 2]
            Bx = B_xf[nxc[0] % 2]
            col = g * 4 + s
            nxc[0] += 1
            cx.dma("sp", xb[:], x[tk:tk + 128, :], writes=[Bx])
            cx.op("act", lambda e: e.activation(out=junk[:], in_=xb[:], func=AF.Square, accum_out=ssq[:, col:col + 1]),
                  reads=[Bx], writes=[B_junk, B_ss[col]])
            cx.op("act", lambda e: e.activation(out=rst[:, col:col + 1], in_=ssq[:, col:col + 1],
                                                func=AF.Sqrt, scale=1.0 / D, bias=EPS),
                  reads=[B_ss[col]], writes=[B_ss[col]])
            cx.op("dve", lambda e: e.reciprocal(out=rst[:, col:col + 1], in_=rst[:, col:col + 1]),
                  reads=[B_ss[col]], writes=[B_ss[col]])
            cx.op("dve", lambda e: e.tensor_scalar(out=xs[g][:, s, :], in0=xb[:], scalar1=rst[:, col:col + 1],
                                                   scalar2=None, op0=ALU.mult),
                  reads=[Bx, B_ss[col]], writes=[B_xs[g][s]])

        for g in range(2):
            for s in range(4):
                xprep_sub(0, g, s)
        for ti in range(8):
            ttype = OWN if ti < 4 else (HALO if ti == 4 else FAR)
            t0 = ti * 1024
            for g in range(2):
                for kc in range(16):
                    pb = pt[kc % 4]
                    Bp = B_pt[kc % 4]
                    fns = [lambda e, s=s, kc=kc, pb=pb, g=g: e.transpose(out=pb[:, s * 128:(s + 1) * 128],
                                                                        in_=xs[g][:, s, kc * 128:(kc + 1) * 128],
                                                                        identity=ident[:]) for s in range(4)]
                    cx.group("pe", fns, reads=B_xs[g] + [B_ident], writes=[Bp])
                    cx.op("act", lambda e, kc=kc, pb=pb, g=g: e.activation(
                        out=hT[:, kc, g * 512:(g + 1) * 512], in_=pb[:], func=AF.Identity,
                        scale=sc1[:, kc:kc + 1], bias=shf[:, kc:kc + 1]),
                          reads=[Bp, B_mod], writes=[B_hT[kc][g]])
            nxt = [(g, s) for g in range(2) for s in range(4)] if ti + 1 < 8 else []
            ngroups_t = sum(1 for g_ in GROUPS if ttype in g_[5])
            per_group = -(-len(nxt) // ngroups_t) if nxt else 0
            first_group = True
            for (c0, ncl, kind, dest, doff, types) in GROUPS:
                if ttype not in types:
                    continue
                w = wg[nwg % 3]
                Bw = B_wg[nwg % 3]
                nwg += 1
                gi_ = [g_[0] for g_ in GROUPS].index(c0)
                if ti == 0:
                    for hf in range((ncl + 255) // 256):
                        wd = min(256, ncl - hf * 256)
                        fb = wf32[nwf[0] % 2]
                        Bfb = B_wf32[nwf[0] % 2]
                        cx.dma("sp", fb[:, :, 0:wd], w_inv[:, :, c0 + hf * 256:c0 + hf * 256 + wd], writes=[Bfb])
                        cx.op("dve", lambda e, fb=fb, wd=wd, hf=hf, w=w: e.tensor_copy(
                            out=w[:, :, hf * 256:hf * 256 + wd], in_=fb[:, :, 0:wd]), reads=[Bfb], writes=[Bw])
                        nwf[0] += 1
                    cx.dma("pool", wbv[:, :, c0:c0 + ncl], w[:, :, 0:ncl], reads=[Bw], writes=[B_wbfg[gi_]])
                else:
                    cx.dma("sp", w[:, :, 0:ncl], wbv[:, :, c0:c0 + ncl], reads=[B_wbfg[gi_]], writes=[Bw])
                if kind == "fm":
                    st = stg[nstg % 3]
                    Bs = B_stg[nstg % 3]
                    nstg += 1
                    for c in range(4):
                        for th in range(2):
                            pb = pp[npp % 4]
                            Bp = B_pp[npp % 4]
                            npp += 1
                            fns = [lambda e, kc=kc, c=c, th=th, pb=pb, w=w: e.matmul(
                                pb[:, :], lhsT=w[:, kc, c * 128:(c + 1) * 128], rhs=hT[:, kc, th * 512:(th + 1) * 512],
                                start=(kc == 0), stop=(kc == 15)) for kc in range(16)]
                            cx.group("pe", fns, reads=[Bw] + [B_hT[kc][th] for kc in range(16)], writes=[Bp])
                            o = st[:, c * 1024 + th * 512: c * 1024 + (th + 1) * 512]
                            if nev % 2 == 0:
                                cx.op("act", lambda e, o=o, pb=pb: e.copy(out=o, in_=pb[:, :]), reads=[Bp], writes=[Bs])
                            else:
                                cx.op("dve", lambda e, o=o, pb=pb: e.tensor_copy(out=o, in_=pb[:, :]), reads=[Bp], writes=[Bs])
                            nev += 1
                    cx.dma("pool", dest[doff:doff + 512, t0:t0 + 1024].rearrange("(c p) t -> p c t", p=128),
                           st[:, :].rearrange("p (c t) -> p c t", c=4), reads=[Bs], writes=[B_scr])
                elif kind == "tm":
                    st = stg[nstg % 3]
                    Bs = B_stg[nstg % 3]
                    nstg += 1
                    for s in range(8):
                        pb = pp[npp % 4]
                        Bp = B_pp[npp % 4]
                        npp += 1
                        fns = [lambda e, kc=kc, s=s, pb=pb, w=w: e.matmul(
                            pb[:, :], lhsT=hT[:, kc, s * 128:(s + 1) * 128], rhs=w[:, kc, 0:512],
                            start=(kc == 0), stop=(kc == 15)) for kc in range(16)]
                        cx.group("pe", fns, reads=[Bw] + [B_hT[kc][s // 4] for kc in range(16)], writes=[Bp])
                        o = st[:, s * 512:(s + 1) * 512]
                        if nev % 2 == 0:
                            cx.op("act", lambda e, o=o, pb=pb: e.copy(out=o, in_=pb[:, :]), reads=[Bp], writes=[Bs])
                        else:
                            cx.op("dve", lambda e, o=o, pb=pb: e.tensor_copy(out=o, in_=pb[:, :]), reads=[Bp], writes=[Bs])
                        nev += 1
                    roff = 1024 if dest is vtm else 0
                    cx.dma("pool", dest[roff + t0:roff + t0 + 1024, doff:doff + 512].rearrange("(s p) c -> p s c", p=128),
                           st[:, :].rearrange("p (s c) -> p s c", s=8), reads=[Bs], writes=[B_scr])
                else:
                    for th in range(2):
                        pb = pp[npp % 4]
                        Bp = B_pp[npp % 4]
                        npp += 1
                        fns = [lambda e, kc=kc, th=th, pb=pb, w=w: e.matmul(
                            pb[0:32, :], lhsT=w[:, kc, 0:32], rhs=hT[:, kc, th * 512:(th + 1) * 512],
                            start=(kc == 0), stop=(kc == 15)) for kc in range(16)]
                        cx.group("pe", fns, reads=[Bw] + [B_hT[kc][th] for kc in range(16)], writes=[Bp])
                        cx.op("dve", lambda e, th=th, pb=pb: e.tensor_copy(out=lrstg[:, th * 512:(th + 1) * 512],
                                                                           in_=pb[0:32, :]), reads=[Bp], writes=[B_lrstg])
                    cx.dma("pool", lrT[:, t0:t0 + 1024], lrstg[:], reads=[B_lrstg], writes=[B_scr])
                for _ in range(per_group):
                    if nxt:
                        xprep_sub(ti + 1, *nxt.pop(0))
            while nxt:
                xprep_sub(ti + 1, *nxt.pop(0))
        cx.barrier()
    if stop("pA"):
        return finish()

    with contextlib.ExitStack() as es:
        def sb(shape, dt, name):
            return es.enter_context(nc.sbuf_tensor(name, list(shape), dt))

        def ps(shape, dt, name):
            return es.enter_context(nc.psum_tensor(name, list(shape), dt))

        qn = sb([128, TOWN], BF16, "b_qn")
        kn = sb([128, 64 + TOWN + 1024], BF16, "b_kn")
        qp = sb([128, TOWN], BF16, "b_qp")
        kp = sb([128, 6144], BF16, "b_kp")
        vraw = [sb([128, 48, 128], BF16, "b_vraw%d" % i) for i in range(2)]
        vaug = [sb([128, 48, 128], BF16, "b_vaug%d" % i) for i in range(2)]
        badd = [[sb([128, 1024], F32, "b_badd%d_%d" % (pi, hd)) for hd in range(2)] for pi in range(3)]
        braw = sb([128, 256], F32, "b_braw")
        amask = sb([128, 512], F32, "b_amask")
        negm = sb([128, 512], F32, "b_negm")
        sadd = [sb([128, 512], F32, "b_sadd%d" % i) for i in range(4)]
        pT = [sb([128, 512], BF16, "b_pT%d" % i) for i in range(4)]
        acc = [sb([128, TOWN], F32, "b_acc%d" % i) for i in range(2)]
        densh = sb([64, 2048], F32, "b_densh")
        agt = sb([64, 2048], BF16, "b_ag")
        sil = sb([64, 2048], F32, "b_sil")
        aout = sb([64, 2048], BF16, "b_aout")
        ps_s = [ps([128, 512], F32, "b_pss%d" % i) for i in range(4)]
        ps_o = [ps([128, 256], F32, "b_pso%d" % i) for i in range(4)]
        B_qn, B_kn, B_qp, B_kp = Buf(), Buf(), Buf(), Buf()
        B_vraw = [Buf(), Buf()]
        B_vaug = [Buf(), Buf()]
        ngrp = {"v": 0, "a": 0, "ss": 0, "so": 0}
        B_badd = [[Buf(), Buf()] for _ in range(3)]
        B_braw, B_am = Buf(), Buf()
        B_sadd = [Buf() for _ in range(4)]
        B_pT = [Buf() for _ in range(4)]
        B_acc = [Buf(), Buf()]
        B_densh, B_ag, B_sil, B_aout = Buf(), Buf(), Buf(), Buf()
        B_pss = [Buf() for _ in range(4)]
        B_pso = [Buf() for _ in range(4)]
        B_mix = Buf()

        cx.dma("sp", amask[:], amask_in[:, :], writes=[B_am])
        cx.dma("sp", negm[:], negm_in[:, :], writes=[B_am])
        cx.op("pool", lambda e: e.memset(kn[:, 0:64], 0.0), writes=[B_kn])
        for i in range(2):
            cx.op("pool", lambda e, i=i: e.memset(vaug[i][:, :, 64:128], 1.0), writes=[B_vaug[i]])
        pending_tail = [None]

        def tail(hp, hd):
            frow = (hp * 2 + hd) * 64
            cx.op("act", lambda e: e.activation(out=acc[hd][64:128, :], in_=acc[hd][64:128, :], func=AF.Ln),
                  writes=[B_acc[hd]])
            cx.op("act", lambda e: e.activation(out=acc[hd][64:128, :], in_=acc[hd][64:128, :], func=AF.Exp, scale=-1.0),
                  writes=[B_acc[hd]])
            for hf in range(2):
                cs = slice(hf * 2048, (hf + 1) * 2048)
                cx.dma("sp", densh[:, :], acc[hd][64:128, cs], reads=[B_acc[hd]], writes=[B_densh])
                cx.dma("sp", agt[:, :], agT[frow:frow + 64, cs], writes=[B_ag])
                cx.op("act", lambda e: e.activation(out=sil[:], in_=agt[:], func=AF.Silu), reads=[B_ag], writes=[B_sil])
                cx.op("pool", lambda e, cs=cs: e.tensor_tensor(out=acc[hd][0:64, cs], in0=acc[hd][0:64, cs],
                                                               in1=densh[:, :], op=ALU.mult),
                      reads=[B_densh], writes=[B_acc[hd]])
                cx.op("pool", lambda e, cs=cs: e.tensor_tensor(out=aout[:], in0=acc[hd][0:64, cs], in1=sil[:], op=ALU.mult),
                      reads=[B_sil, B_acc[hd]], writes=[B_aout])
                cx.dma("pool", mixT[frow:frow + 64, cs], aout[:], reads=[B_aout], writes=[B_mix])

        for hp in range(8):
            cx.dma("sp", qn[:], qT[hp * 128:(hp + 1) * 128, :], writes=[B_qn])
            cx.dma("sp", kn[:, 64:64 + TOWN + 1024], kT[hp * 128:(hp + 1) * 128, :], writes=[B_kn])
            for pi in range(3):
                for hd in range(2):
                    cx.dma("sp", braw[:], btab[pi * 16 + hp * 2 + hd, :, :], writes=[B_braw])
                    bt = badd[pi][hd]
                    cx.op("dve", lambda e, bt=bt: e.tensor_tensor(out=bt[:, 0:256], in0=braw[:], in1=amask[:, 0:256], op=ALU.mult),
                          reads=[B_braw, B_am], writes=[B_badd[pi][hd]])
                    cx.op("dve", lambda e, bt=bt: e.tensor_tensor(out=bt[:, 256:512], in0=braw[:], in1=amask[:, 256:512], op=ALU.mult),
                          reads=[B_braw, B_am], writes=[B_badd[pi][hd]])
                    cx.op("dve", lambda e, bt=bt: e.tensor_tensor(out=bt[:, 0:512], in0=bt[:, 0:512], in1=negm[:], op=ALU.add),
                          reads=[B_am], writes=[B_badd[pi][hd]])
                    cx.op("dve", lambda e, bt=bt: e.tensor_copy(out=bt[:, 512:768], in_=bt[:, 256:512]), writes=[B_badd[pi][hd]])
                    cx.op("dve", lambda e, bt=bt: e.tensor_copy(out=bt[:, 768:1024], in_=bt[:, 256:512]), writes=[B_badd[pi][hd]])
            items = []
            pat_hooks = []
            for pi, d in enumerate((1, 4, 16)):
                Mown = TOWN // d
                seg = Mown + 128
                nq = Mown // 128
                nch = nq + 1
                if d == 1:
                    qv, kv = qn, kn
                    Bq, Bk = B_qn, B_kn
                else:
                    qv, kv = qp, kp
                    Bq, Bk = B_qp, B_kp
                vb = ngrp["v"] % 2
                ngrp["v"] += 1

                def pattern_vload(d=d, nch=nch, vb=vb):
                    base = 1024 - 64 * d
                    vsrc = vtm[base:base + nch * 128 * d, hp * 128:(hp + 1) * 128].rearrange("(c p r) f -> r p c f", p=128, r=d)
                    for r in range(d):
                        cx.dma("sp", vraw[vb][:, r * nch:(r + 1) * nch, :], vsrc[r], writes=[B_vraw[vb]])

                def pattern_pre(d=d, Mown=Mown, seg=seg, nch=nch, vb=vb):
                    if d > 1:
                        cx.op("act", lambda e: e.copy(out=qp[:, :].rearrange("p (r m) -> p r m", r=d),
                                                      in_=qn[:, :].rearrange("p (m r) -> p r m", r=d)),
                              reads=[B_qn], writes=[B_qp])
                        kpv = kp[:, 0:d * seg].rearrange("p (r s) -> p r s", r=d)
                        cx.op("pool", lambda e: e.memset(kpv[:, :, 0:64], 0.0), writes=[B_kp])
                        cx.op("act", lambda e: e.copy(
                            out=kpv[:, :, 64:64 + Mown + 64],
                            in_=kn[:, 64:64 + (Mown + 64) * d].rearrange("p (m r) -> p r m", r=d)),
                              reads=[B_kn], writes=[B_kp])
                pat_hooks.append((pattern_pre, pattern_vload))

                for hd in range(2):
                    lo, hi = hd * 64, (hd + 1) * 64
                    ab = ngrp["a"] % 2
                    ngrp["a"] += 1

                    def group_pre(lo=lo, hi=hi, d=d, nch=nch, vb=vb, ab=ab):
                        cx.op("act", lambda e: e.copy(out=vaug[ab][:, 0:d * nch, 0:64], in_=vraw[vb][:, 0:d * nch, lo:hi]),
                              reads=[B_vraw[vb]], writes=[B_vaug[ab]])

                    bt = badd[pi][hd]
                    first = True
                    for r in range(d):
                        for i2 in range(nq // 2):
                            pre = []
                            if first:
                                pre.append(group_pre)
                                first = False
                            st = {}

                            def s1(st=st, r=r, i2=i2, lo=lo, hi=hi, qv=qv, kv=kv, Bq=Bq, Bk=Bk, Mown=Mown, seg=seg, bt=bt, pi=pi, hd=hd):
                                k_ = ngrp["ss"] % 4
                                ngrp["ss"] += 1
                                st["k"] = k_
                                pss, sa, pt_ = ps_s[k_], sadd[k_], pT[k_]
                                fns = []
                                q0 = r * Mown + 256 * i2
                                kb0 = r * seg + 256 * i2
                                for j, (qa, qb, oa, ob) in enumerate(((0, 128, 0, 128), (0, 256, 128, 384), (128, 256, 384, 512))):
                                    fns.append(lambda e, j=j, qa=qa, qb=qb, oa=oa, ob=ob: e.matmul(
                                        pss[:, oa:ob], lhsT=kv[lo:hi, kb0 + 128 * j:kb0 + 128 * j + 128],
                                        rhs=qv[lo:hi, q0 + qa:q0 + qb], start=True, stop=True))
                                cx.group("pe", fns, reads=[Bq, Bk], writes=[B_pss[k_]])
                                boff = 0 if i2 == 0 else 512
                                cx.op("dve", lambda e: e.scalar_tensor_tensor(
                                    out=sa[:], in0=pss[:, :], scalar=0.125, in1=bt[:, boff:boff + 512], op0=ALU.mult, op1=ALU.add),
                                      reads=[B_pss[k_], B_badd[pi][hd]], writes=[B_sadd[k_]])
                                cx.op("act", lambda e: e.activation(out=pt_[:], in_=sa[:], func=AF.Exp),
                                      reads=[B_sadd[k_]], writes=[B_pT[k_]])

                            def s2(st=st, r=r, i2=i2, d=d, nch=nch, ab=ab, pi=pi, hd=hd):
                                k_ = st["k"]
                                pt_ = pT[k_]
                                o_ = ngrp["so"] % 4
                                ngrp["so"] += 1
                                pso = ps_o[o_]
                                fns = []
                                ch0 = r * nch + 2 * i2
                                for j, (pa, pb, oa, ob) in enumerate(((0, 128, 0, 128), (128, 384, 0, 256), (384, 512, 128, 256))):
                                    fns.append(lambda e, j=j, pa=pa, pb=pb, oa=oa, ob=ob: e.matmul(
                                        pso[:, oa:ob], lhsT=vaug[ab][:, ch0 + j, :], rhs=pt_[:, pa:pb],
                                        start=(j == 0), stop=(j == 2), skip_group_check=True))
                                cx.group("pe", fns, reads=[B_vaug[ab], B_pT[k_]], writes=[B_pso[o_]])
                                a0 = r + 256 * i2 * d
                                av = acc[hd][:, a0:a0 + 255 * d + 1:d] if d > 1 else acc[hd][:, a0:a0 + 256]
                                if pi == 0:
                                    cx.op("act", lambda e: e.copy(out=av, in_=pso[:, :]), reads=[B_pso[o_]], writes=[B_acc[hd]])
                                else:
                                    cx.op("dve", lambda e: e.tensor_tensor(out=av, in0=pso[:, :], in1=av, op=ALU.add),
                                          reads=[B_pso[o_]], writes=[B_acc[hd]])

                            items.append((pre, s1, s2))
            items[0][0][0:0] = [pat_hooks[0][0], pat_hooks[0][1]]
            items[2][0][0:0] = [pat_hooks[1][0], pat_hooks[1][1]]
            items[20][0][0:0] = [pat_hooks[2][1]]
            items[64][0][0:0] = [pat_hooks[2][0]]
            if pending_tail[0] is not None:
                items[4][0].append(pending_tail[0])
                pending_tail[0] = None
            items[84][0].append(lambda hp=hp: tail(hp, 0))
            pending_tail[0] = (lambda hp=hp: tail(hp, 1))
            LOOK = 3
            for j in range(len(items) + LOOK):
                if j < len(items):
                    for f in items[j][0]:
                        f()
                    items[j][1]()
                if j - LOOK >= 0:
                    items[j - LOOK][2]()
        if pending_tail[0] is not None:
            pending_tail[0]()
        cx.barrier()
    if stop("pB"):
        return finish()

    LN_QS = -2.4260151319598084
    with contextlib.ExitStack() as es:
        def sb(shape, dt, name):
            return es.enter_context(nc.sbuf_tensor(name, list(shape), dt))

        def ps(shape, dt, name):
            return es.enter_context(nc.psum_tensor(name, list(shape), dt))

        ps_o = ps([128, 1024], F32, "c_pso")
        ps_z = ps([128, 512], F32, "c_psz")
        ps_gt = ps([128, 1024], BF16, "c_psgt")
        ps_Pb = [ps([128, 512], F32, "c_psP%d" % i) for i in range(2)]
        ps_P = [ps_Pb[0][:, 0:256], ps_Pb[0][:, 256:512], ps_Pb[1][:, 0:256], ps_Pb[1][:, 256:512]]
        ps_ab = ps([128, 512], F32, "c_psa")
        ps_a = [ps_ab[:, i * 128:(i + 1) * 128] for i in range(4)]
        if DEBUG.get("skipo"):
            ps_a = [ps_o[:, i * 128:(i + 1) * 128] for i in range(4)]
        ps_tb = ps([128, 512], BF16, "c_pst")
        ps_t = [ps_tb[:, i * 128:(i + 1) * 128] for i in range(4)]
        upw = [sb([96, 512], F32, "c_up%d" % i) for i in range(2)]
        upb3 = [sb([96, 512], BF16, "c_upb%d" % i) for i in range(2)]
        uptmp = sb([96, 512], BF16, "c_uptmp")
        lr3 = [sb([96, 512], BF16, "c_lr3_%d" % i) for i in range(2)]
        lrtmp = sb([96, 512], BF16, "c_lrtmp")
        ngb = sb([128, 8], F32, "c_ngb")
        gmask = [sb([128, 128], F32, "c_mask%d" % i) for i in range(2)]
        gainb = sb([128, 1024], F32, "c_gainb")
        lrb = [sb([96, 512], F32, "c_lrb%d" % i) for i in range(2)]
        kblk = [sb([128, 4, 512], BF16, "c_kblk%d" % i) for i in range(2)]
        qblk = [sb([128, 4, 512], BF16, "c_qblk%d" % i) for i in range(2)]
        gvblk = [sb([128, 4, 1024], BF16, "c_gvblk%d" % i) for i in range(2)]
        ggblk = [sb([128, 4, 1024], BF16, "c_ggblk%d" % i) for i in range(2)]
        e1 = [sb([128, 512], F32, "c_e1_%d" % i) for i in range(2)]
        nlg = sb([128, 4, 512], F32, "c_nlg")
        cb = sb([128, 4, 512], F32, "c_cb")
        onesf = sb([128, 128], F32, "c_ones")
        ekb = sb([128, 4, 512], F32, "c_ekb")
        eb = sb([128, 4, 512], F32, "c_eb")
        ebl = [sb([128, 4, 4], F32, "c_ebl%d" % i) for i in range(2)]
        kbT = [sb([128, 4, 512], BF16, "c_kbT%d" % i) for i in range(2)]
        qfT = [sb([128, 4, 512], BF16, "c_qfT%d" % i) for i in range(2)]
        kbtm = [sb([128, 128], BF16, "c_kbtm%d" % i) for i in range(4)]
        attm = [sb([128, 128], BF16, "c_attm%d" % i) for i in range(4)]
        U = [sb([128, 256], F32, "c_U%d" % i) for i in range(4)]
        Sbf = [[sb([128, 256], BF16, "c_Sbf%d_%d" % (j, i)) for i in range(4)] for j in range(2)]
        lnqs = sb([128, 1], F32, "c_lnqs")
        obuf = [sb([128, 1024], F32, "c_obuf%d" % i) for i in range(2)]
        otot = sb([128, 1024], F32, "c_otot")
        junkc = sb([128, 256], BF16, "c_junk")
        ss4 = sb([128, 4], F32, "c_ss4")
        silg = sb([128, 1024], F32, "c_silg")
        gout = sb([128, 1024], BF16, "c_gout")
        gstg = [sb([128, 8, 512], BF16, "c_gstg%d" % i) for i in range(2)]
        B_const = Buf()
        B_lrb = [Buf(), Buf()]
        B_kblk = [Buf(), Buf()]
        B_qblk = [Buf(), Buf()]
        B_gvblk = [Buf(), Buf()]
        B_ggblk = [Buf(), Buf()]
        B_e1 = [Buf(), Buf()]
        B_psz = Buf()
        B_nlg = [Buf() for _ in range(4)]
        B_cb = [Buf() for _ in range(4)]
        B_ekb = [Buf() for _ in range(4)]
        B_eb = [Buf() for _ in range(4)]
        B_ebl = [[Buf() for _ in range(4)] for _ in range(2)]
        B_kbT = [[Buf() for _ in range(4)] for _ in range(2)]
        B_qfT = [[Buf() for _ in range(4)] for _ in range(2)]
        B_kbtm = [Buf() for _ in range(4)]
        B_attm = [Buf() for _ in range(4)]
        B_U = [Buf() for _ in range(4)]
        B_Sbf = [[Buf() for _ in range(4)] for _ in range(2)]
        B_psP = [Buf(), Buf()]
        B_psa = Buf()
        B_pst = Buf()
        B_pso = [Buf(), Buf()]
        B_psgt = Buf()
        B_obuf = [Buf(), Buf()]
        B_otot, B_ss4, B_silg, B_gout, B_junkc = Buf(), Buf(), Buf(), Buf(), Buf()
        B_gstg = [Buf(), Buf()]
        B_obs = Buf()
        B_mix2 = Buf()

        for i, upsrc in enumerate((up_f, up_b)):
            cx.op("pool", lambda e, i=i: e.memset(upb3[i][:], 0.0), writes=[B_const])
            cx.op("pool", lambda e, i=i: e.memset(lr3[i][:], 0.0), writes=[B_lrb[i]])
            for g in range(3):
                cx.dma("sp", upw[i][g * 32:g * 32 + 16, :], upsrc[:, :], writes=[B_const])
            cx.op("dve", lambda e, i=i: e.tensor_copy(out=upb3[i][0:16, :], in_=upw[i][0:16, :]), reads=[B_const], writes=[B_const])
            cx.op("dve", lambda e, i=i: e.tensor_copy(out=upb3[i][32:48, :], in_=upw[i][32:48, :]), reads=[B_const], writes=[B_const])
            cx.op("dve", lambda e, i=i: e.tensor_copy(out=uptmp[64:80, :], in_=upw[i][64:80, :]), reads=[B_const], writes=[B_const])
            cx.op("dve", lambda e, i=i: e.tensor_tensor(out=upb3[i][64:80, :], in0=upw[i][64:80, :], in1=uptmp[64:80, :], op=ALU.subtract),
                  reads=[B_const], writes=[B_const])
        cx.dma("sp", ngb[:, 0:4], gb_f.rearrange("(h p) -> p h", p=128), writes=[B_const], allow_slow_non_contiguous=True)
        cx.dma("sp", ngb[:, 4:8], gb_b.rearrange("(h p) -> p h", p=128), writes=[B_const], allow_slow_non_contiguous=True)
        cx.dma("sp", gmask[0][:], maskf_in[:, :], writes=[B_const])
        cx.dma("sp", gmask[1][:], maskb_in[:, :], writes=[B_const])
        cx.dma("sp", gainb[:], gnorm.partition_broadcast(128), writes=[B_const])
        cx.op("pool", lambda e: e.tensor_scalar(out=ngb[:], in0=ngb[:], scalar1=-1.0, scalar2=None, op0=ALU.mult),
              reads=[B_const], writes=[B_const])
        cx.op("pool", lambda e: e.memset(onesf[:], 1.0), writes=[B_const])
        cx.op("pool", lambda e: e.memset(lnqs[:], LN_QS), writes=[B_const])

        cnt = {"ob": 0}

        def gla_pass(dirn):
            final = (dirn == 0)
            for h in range(4):
                cx.op("pool", lambda e, h=h: e.memset(U[h][:], 0.0), writes=[B_U[h]])
                cx.op("pool", lambda e, h=h: e.memset(Sbf[0][h][:], 0.0), writes=[B_Sbf[0][h]])
            blocks = list(range(8)) if dirn == 0 else list(range(15, -1, -1))
            chunks = [0, 1, 2, 3] if dirn == 0 else [3, 2, 1, 0]
            pos = 127 if dirn == 0 else 0

            def prepA(blk):
                own = blk < 8
                t0 = blk * 512
                bi = blk % 2
                for g in range(3):
                    cx.dma("sp", lrb[bi][g * 32:g * 32 + 16, :], lrT[dirn * 16:(dirn + 1) * 16, t0:t0 + 512], writes=[B_lrb[bi]])
                cx.op("dve", lambda e: e.tensor_copy(out=lr3[bi][0:16, :], in_=lrb[bi][0:16, :]), reads=[B_lrb[bi]], writes=[B_lrb[bi]])
                cx.op("dve", lambda e: e.tensor_copy(out=lr3[bi][64:80, :], in_=lrb[bi][64:80, :]), reads=[B_lrb[bi]], writes=[B_lrb[bi]])
                cx.op("dve", lambda e: e.tensor_copy(out=lrtmp[32:48, :], in_=lrb[bi][32:48, :]), reads=[B_lrb[bi]], writes=[B_lrb[bi]])
                cx.op("dve", lambda e: e.tensor_tensor(out=lr3[bi][32:48, :], in0=lrb[bi][32:48, :], in1=lrtmp[32:48, :], op=ALU.subtract),
                      reads=[B_lrb[bi]], writes=[B_lrb[bi]])
                cx.dma("sp", kblk[bi][:], gkT[:, t0:t0 + 512].rearrange("(h p) t -> p h t", p=128), writes=[B_kblk[bi]])
                cx.dma("sp", gvblk[bi][:], gvs[t0:t0 + 512, :].rearrange("(c p) f -> p c f", p=128), writes=[B_gvblk[bi]])
                if own:
                    cx.dma("sp", qblk[bi][:], gqT[:, t0:t0 + 512].rearrange("(h p) t -> p h t", p=128), writes=[B_qblk[bi]])
                if own and final:
                    cx.dma("sp", ggblk[bi][:], ggs[t0:t0 + 512, :].rearrange("(c p) f -> p c f", p=128), writes=[B_ggblk[bi]])
                for h in range(4):
                    cx.group("pe", [lambda e, h=h: e.matmul(ps_z[:, :], lhsT=upb3[dirn][:, h * 128:(h + 1) * 128],
                                                           rhs=lr3[bi][:, :], start=True, stop=True)],
                             reads=[B_const, B_lrb[bi]], writes=[B_psz])
                    zi = h % 2
                    cx.op("act", lambda e, h=h, zi=zi: e.activation(out=e1[zi][:], in_=ps_z[:, :], func=AF.Exp, scale=-1.0,
                                                                   bias=ngb[:, dirn * 4 + h:dirn * 4 + h + 1]),
                          reads=[B_psz, B_const], writes=[B_e1[zi]])
                    cx.op("act", lambda e, h=h, zi=zi: e.activation(out=nlg[:, h, :], in_=e1[zi][:], func=AF.Ln, bias=1.0),
                          reads=[B_e1[zi]], writes=[B_nlg[h]])
                    for c in range(4):
                        o_ = cb[:, h, c * 128:(c + 1) * 128]
                        i_ = nlg[:, h, c * 128:(c + 1) * 128]
                        on_ = onesf[:, :]
                        if dirn == 1:
                            o_, i_, on_ = o_[:, ::-1], i_[:, ::-1], on_[:, ::-1]
                        cx.op("dve", lambda e, o_=o_, i_=i_, on_=on_: e.tensor_tensor_scan(
                            out=o_, data0=on_, data1=i_, initial=0.0, op0=ALU.mult, op1=ALU.add),
                              reads=[B_nlg[h], B_const], writes=[B_cb[h]])
                    cx.op("act", lambda e, h=h: e.activation(out=ebl[bi][:, h, :], in_=cb[:, h, pos:512:128], func=AF.Exp,
                                                             scale=-1.0 / 16.0),
                          reads=[B_cb[h]], writes=[B_ebl[bi][h]])
                    cx.op("act", lambda e, h=h: e.activation(out=ekb[:, h, :], in_=cb[:, h, :], func=AF.Exp, scale=1.0 / 16.0),
                          reads=[B_cb[h]], writes=[B_ekb[h]])
                    cx.op("pool", lambda e, h=h: e.tensor_tensor(out=kbT[bi][:, h, :], in0=kblk[bi][:, h, :], in1=ekb[:, h, :],
                                                                 op=ALU.mult),
                          reads=[B_kblk[bi], B_ekb[h]], writes=[B_kbT[bi][h]])
                    if own:
                        cx.op("act", lambda e, h=h: e.activation(out=eb[:, h, :], in_=cb[:, h, :], func=AF.Exp, scale=-1.0 / 16.0,
                                                                 bias=lnqs[:, 0:1]),
                              reads=[B_cb[h], B_const], writes=[B_eb[h]])
                        cx.op("pool", lambda e, h=h: e.tensor_tensor(out=qfT[bi][:, h, :], in0=qblk[bi][:, h, :], in1=eb[:, h, :],
                                                                     op=ALU.mult),
                              reads=[B_qblk[bi], B_eb[h]], writes=[B_qfT[bi][h]])

            prepA(blocks[0])
            eprev = [onesf[:, 0:1]] * 4
            Beprev = [B_const] * 4
            par = 0
            for bidx, blk in enumerate(blocks):
                own = blk < 8 and not DEBUG.get("noown")
                t0 = blk * 512
                bi = blk % 2
                for cidx, c in enumerate(chunks):
                    csl = slice(c * 128, (c + 1) * 128)
                    if own:
                        cx.group("pe", [lambda e, h=h: e.matmul(ps_a[h], lhsT=kbT[bi][:, h, csl], rhs=qfT[bi][:, h, csl],
                                                               start=True, stop=True) for h in range(4)],
                                 reads=B_kbT[bi] + B_qfT[bi], writes=[B_psa])
                    cx.group("pe", [lambda e, h=h: e.transpose(out=ps_t[h], in_=kbT[bi][:, h, csl], identity=ident[:])
                                    for h in range(4)], reads=B_kbT[bi] + [B_ident], writes=[B_pst])
                    for h in range(4):
                        cx.op("act", lambda e, h=h: e.copy(out=kbtm[h][:], in_=ps_t[h]), reads=[B_pst], writes=[B_kbtm[h]])
                    if own:
                        for h in range(4):
                            cx.op("dve", lambda e, h=h: e.tensor_tensor(out=attm[h][:], in0=ps_a[h], in1=gmask[dirn][:], op=ALU.mult),
                                  reads=[B_psa, B_const], writes=[B_attm[h]])
                    for hh in range(2):
                        cx.group("pe", [lambda e, h=h: e.matmul(ps_P[h], lhsT=kbtm[h][:, :],
                                                               rhs=gvblk[bi][:, c, h * 256:(h + 1) * 256], start=True, stop=True)
                                        for h in (2 * hh, 2 * hh + 1)],
                                 reads=[B_kbtm[2 * hh], B_kbtm[2 * hh + 1], B_gvblk[bi]], writes=[B_psP[hh]])
                    if own:
                        for hh in range(2):
                            fns = []
                            for h in (2 * hh, 2 * hh + 1):
                                vsl = slice(h * 256, (h + 1) * 256)
                                fns.append(lambda e, h=h, vsl=vsl: e.matmul(ps_o[:, vsl], lhsT=attm[h][:, :], rhs=gvblk[bi][:, c, vsl],
                                                                            start=True, stop=False))
                                fns.append(lambda e, h=h, vsl=vsl: e.matmul(ps_o[:, vsl], lhsT=qfT[bi][:, h, csl], rhs=Sbf[par][h][:, :],
                                                                            start=False, stop=True))
                            cx.group("pe", fns, reads=[B_attm[2 * hh], B_attm[2 * hh + 1], B_gvblk[bi], B_qfT[bi][2 * hh],
                                                       B_qfT[bi][2 * hh + 1], B_Sbf[par][2 * hh], B_Sbf[par][2 * hh + 1]],
                                     writes=[B_pso[hh]])
                    for h in range(4):
                        cx.op("dve", lambda e, h=h, ep=eprev[h]: e.scalar_tensor_tensor(
                            out=U[h][:], in0=U[h][:], scalar=ep, in1=ps_P[h], op0=ALU.mult, op1=ALU.add),
                              reads=[B_psP[h // 2], Beprev[h]], writes=[B_U[h]])
                        eprev[h] = ebl[bi][:, h, c:c + 1]
                        Beprev[h] = B_ebl[bi][h]
                    for h in range(4):
                        cx.op("act", lambda e, h=h: e.activation(out=Sbf[1 - par][h][:], in_=U[h][:], func=AF.Copy,
                                                                 scale=ebl[bi][:, h, c:c + 1]),
                              reads=[B_U[h], B_ebl[bi][h]], writes=[B_Sbf[1 - par][h]])
                    par = 1 - par
                    if cidx == 0 and bidx + 1 < len(blocks):
                        prepA(blocks[bidx + 1])
                    if not own or DEBUG.get("skipfin"):
                        continue
                    tk = t0 + c * 128
                    oi = cnt["ob"] % 2
                    cnt["ob"] += 1
                    if not final:
                        cx.op("act", lambda e, oi=oi: e.copy(out=obuf[oi][:], in_=ps_o[:, :]), reads=B_pso, writes=[B_obuf[oi]])
                        cx.dma("pool", obs[tk:tk + 128, :], obuf[oi][:], reads=[B_obuf[oi]], writes=[B_obs])
                        continue
                    cx.dma("sp", obuf[oi][:], obs[tk:tk + 128, :], writes=[B_obuf[oi]])
                    cx.op("dve", lambda e, oi=oi: e.tensor_tensor(out=otot[:], in0=ps_o[:, :], in1=obuf[oi][:], op=ALU.add),
                          reads=B_pso + [B_obuf[oi]], writes=[B_otot])
                    for h in range(4):
                        vsl = slice(h * 256, (h + 1) * 256)
                        cx.op("act", lambda e, h=h, vsl=vsl: e.activation(out=junkc[:], in_=otot[:, vsl], func=AF.Square,
                                                                         accum_out=ss4[:, h:h + 1]),
                              reads=[B_otot], writes=[B_junkc, B_ss4])
                    cx.op("act", lambda e: e.activation(out=ss4[:], in_=ss4[:], func=AF.Sqrt, scale=1.0 / 256.0, bias=EPS),
                          reads=[B_ss4], writes=[B_ss4])
                    cx.op("dve", lambda e: e.reciprocal(out=ss4[:], in_=ss4[:]), reads=[B_ss4], writes=[B_ss4])
                    cx.op("act", lambda e, c=c: e.activation(out=silg[:], in_=ggblk[bi][:, c, :], func=AF.Silu),
                          reads=[B_ggblk[bi]], writes=[B_silg])
                    cx.op("pool", lambda e: e.tensor_tensor(out=silg[:], in0=silg[:], in1=gainb[:], op=ALU.mult),
                          reads=[B_const], writes=[B_silg])
                    for h in range(4):
                        vsl = slice(h * 256, (h + 1) * 256)
                        cx.op("dve", lambda e, h=h, vsl=vsl: e.scalar_tensor_tensor(
                            out=gout[:, vsl], in0=otot[:, vsl], scalar=ss4[:, h:h + 1], in1=silg[:, vsl], op0=ALU.mult, op1=ALU.mult),
                              reads=[B_otot, B_ss4, B_silg], writes=[B_gout])
                    fns = [lambda e, f=f: e.transpose(out=ps_gt[:, f * 128:(f + 1) * 128], in_=gout[:, f * 128:(f + 1) * 128],
                                                      identity=ident[:]) for f in range(8)]
                    cx.group("pe", fns, reads=[B_gout, B_ident], writes=[B_psgt])
                    gi = (blk % 2)
                    cx.op("act", lambda e, gi=gi, c=c: e.copy(out=gstg[gi][:, :, c * 128:(c + 1) * 128],
                                                              in_=ps_gt[:, :].rearrange("p (f t) -> p f t", f=8)),
                          reads=[B_psgt], writes=[B_gstg[gi]])
                if own and final:
                    gi = (blk % 2)
                    cx.dma("pool", mixT[1024:2048, t0:t0 + 512].rearrange("(f p) t -> p f t", p=128), gstg[gi][:],
                           reads=[B_gstg[gi]], writes=[B_mix2])

        gla_pass(1)
        cx.barrier()
        if not stop("pC1"):
            gla_pass(0)
        cx.barrier()
    if stop("pC"):
        return finish()

    with contextlib.ExitStack() as es:
        def sb(shape, dt, name):
            return es.enter_context(nc.sbuf_tensor(name, list(shape), dt))

        def ps(shape, dt, name):
            return es.enter_context(nc.psum_tensor(name, list(shape), dt))

        wog = sb([128, 16, D], BF16, "d_wog")
        gateb = sb([128, D], F32, "d_gateb")
        fgb = sb([128, D], F32, "d_fgb")
        mixb = [sb([128, 16, 512], BF16, "d_mix%d" % i) for i in range(2)]
        xt = [sb([128, D], F32, "d_xt%d" % i) for i in range(3)]
        ot = [sb([128, D], F32, "d_ot%d" % i) for i in range(2)]
        junkd = sb([128, D], BF16, "d_junk")
        ssd = sb([128, 2], F32, "d_ss")
        ps_y = [ps([128, D], F32, "d_psy%d" % i) for i in range(2)]
        B_wog = [Buf() for _ in range(16)]
        B_gb, B_fg = Buf(), Buf()
        B_mixb = [Buf(), Buf()]
        B_xt = [Buf() for _ in range(3)]
        B_ot = [Buf(), Buf()]
        B_junkd = Buf()
        B_ssd = [Buf(), Buf()]
        B_psy = [Buf(), Buf()]
        B_out = Buf()

        cx.dma("sp", gateb[:], modrow[0:1, 2 * D:3 * D].partition_broadcast(128), writes=[B_gb])
        cx.dma("sp", fgb[:], fgain.partition_broadcast(128), writes=[B_fg])
        nxt = 0
        for kc in range(16):
            xi = nxt % 3
            nxt += 1
            cx.dma("sp", xt[xi][:], w_out[kc * 128:(kc + 1) * 128, :], writes=[B_xt[xi]])
            cx.op("dve" if kc % 2 == 0 else "pool", lambda e, kc=kc, xi=xi: e.tensor_tensor(
                out=wog[:, kc, :], in0=xt[xi][:], in1=gateb[:], op=ALU.mult), reads=[B_xt[xi], B_gb], writes=[B_wog[kc]])
        mixv = mixT.rearrange("(k p) t -> p k t", p=128)
        for blk in range(8):
            mi = blk % 2
            cx.dma("sp", mixb[mi][:], mixv[:, :, blk * 512:(blk + 1) * 512], writes=[B_mixb[mi]])
            for s4 in range(4):
                it = blk * 4 + s4
                tk = it * 128
                xi = nxt % 3
                nxt += 1
                pi_ = it % 2
                cx.dma("sp", xt[xi][:], x[tk:tk + 128, :], writes=[B_xt[xi]])
                fns = []
                for n in range(4):
                    for kc in range(16):
                        fns.append(lambda e, n=n, kc=kc, mi=mi, s4=s4, pi_=pi_: e.matmul(
                            ps_y[pi_][:, n * 512:(n + 1) * 512], lhsT=mixb[mi][:, kc, s4 * 128:(s4 + 1) * 128],
                            rhs=wog[:, kc, n * 512:(n + 1) * 512], start=(kc == 0), stop=(kc == 15)))
                cx.group("pe", fns, reads=[B_mixb[mi]] + B_wog, writes=[B_psy[pi_]])
                cx.op("dve", lambda e, xi=xi, pi_=pi_: e.tensor_tensor(out=xt[xi][:], in0=ps_y[pi_][:, :], in1=xt[xi][:], op=ALU.add),
                      reads=[B_psy[pi_]], writes=[B_xt[xi]])
                cx.op("act", lambda e, xi=xi, pi_=pi_: e.activation(out=junkd[:], in_=xt[xi][:], func=AF.Square,
                                                                   accum_out=ssd[:, pi_:pi_ + 1]),
                      reads=[B_xt[xi]], writes=[B_junkd, B_ssd[pi_]])
                cx.op("act", lambda e, pi_=pi_: e.activation(out=ssd[:, pi_:pi_ + 1], in_=ssd[:, pi_:pi_ + 1], func=AF.Sqrt,
                                                             scale=1.0 / D, bias=EPS), writes=[B_ssd[pi_]])
                cx.op("dve", lambda e, pi_=pi_: e.reciprocal(out=ssd[:, pi_:pi_ + 1], in_=ssd[:, pi_:pi_ + 1]), writes=[B_ssd[pi_]])
                cx.op("dve", lambda e, xi=xi, pi_=pi_: e.scalar_tensor_tensor(
                    out=ot[pi_][:], in0=xt[xi][:], scalar=ssd[:, pi_:pi_ + 1], in1=fgb[:], op0=ALU.mult, op1=ALU.mult),
                      reads=[B_xt[xi], B_ssd[pi_], B_fg], writes=[B_ot[pi_]])
                cx.dma("pool", out[tk:tk + 128, :], ot[pi_][:], reads=[B_ot[pi_]], writes=[B_out])
    return finish()


def host_inputs(inputs):
    x = np.asarray(inputs["x"], np.float32)
    c = np.asarray(inputs["c"], np.float32)
    w_cond = np.ascontiguousarray(np.asarray(inputs["w_cond"], np.float32)[0])
    b_cond = np.ascontiguousarray(np.asarray(inputs["b_cond"], np.float32)[0])[None, :]
    w_in = np.asarray(inputs["w_in"], np.float32)[0]
    w_in_sw = np.ascontiguousarray(np.concatenate([w_in[:, :7168], w_in[:, 7184:7200], w_in[:, 7168:7184]], axis=1))
    w_in = np.ascontiguousarray(w_in)
    upf = np.ascontiguousarray(np.asarray(inputs["gla_gate_up_fwd"], np.float32)[0])
    upb = np.ascontiguousarray(np.asarray(inputs["gla_gate_up_bwd"], np.float32)[0])
    gbf = np.ascontiguousarray(np.asarray(inputs["gla_gate_bias_fwd"], np.float32)[0])
    gbb = np.ascontiguousarray(np.asarray(inputs["gla_gate_bias_bwd"], np.float32)[0])
    gnorm = np.ascontiguousarray(np.asarray(inputs["gla_norm_gain"], np.float32)[0])[None, :]
    rel_bias = np.asarray(inputs["rel_bias"], np.float32)
    w_out = np.ascontiguousarray(np.asarray(inputs["w_out"], np.float32)[0])
    fgain = np.ascontiguousarray(np.asarray(inputs["final_gain"], np.float32))[None, :]

    def bucket(rel):
        nb = 16
        max_exact = 8
        n = np.abs(rel)
        large = max_exact + (np.log(np.maximum(n, 1) / max_exact) / np.log(1024 / max_exact) * (nb - max_exact)).astype(np.int32)
        large = np.minimum(large, nb - 1)
        return (np.where(rel > 0, nb, 0) + np.where(n < max_exact, n, large)).astype(np.int32)

    kl = np.arange(128)[:, None]
    ql = np.arange(128)[None, :]
    step1 = kl - 64 - ql
    step2 = kl + 64 - ql
    btabs = []
    for flip in (False, True):
        tabs = np.zeros((3, 16, 128, 256), np.float32)
        for pi, dil in enumerate((1, 4, 16)):
            for si, st in enumerate((step1, step2)):
                rel = st * dil
                if flip:
                    rel = -rel
                tabs[pi, :, :, si * 128:(si + 1) * 128] = np.transpose(rel_bias[bucket(rel)], (2, 0, 1))
        btabs.append(np.ascontiguousarray(tabs.reshape(48, 128, 256)))
    amask = np.zeros((128, 512), np.float32)
    amask[:, 0:128] = (np.abs(step1) <= 64) & (kl >= 64)
    amask[:, 128:256] = (np.abs(step2) <= 64)
    amask[:, 256:384] = (np.abs(step1) <= 64)
    amask[:, 384:512] = (np.abs(step2) <= 64)
    negm = ((amask - 1.0) * 30000.0).astype(np.float32)
    ident = np.eye(128, dtype=np.float32)
    maskf = (kl <= ql).astype(np.float32)
    maskb = (kl >= ql).astype(np.float32)

    maps = []
    for core in range(8):
        b, half = core // 2, core % 2
        xl = x[b] if half == 0 else x[b, ::-1]
        m = {
            "x": np.ascontiguousarray(xl), "c": np.ascontiguousarray(c[b]),
            "w_cond": w_cond, "b_cond": b_cond,
            "w_in": w_in if half == 0 else w_in_sw,
            "up_f": upf if half == 0 else upb, "up_b": upb if half == 0 else upf,
            "gb_f": gbf if half == 0 else gbb, "gb_b": gbb if half == 0 else gbf,
            "gnorm": gnorm, "btab": btabs[half], "w_out": w_out, "fgain": fgain,
            "ident": ident, "maskf": maskf, "maskb": maskb, "amask": amask, "negm": negm,
        }
        maps.append(m)
    return maps


def kernel(**inputs):
    maps = host_inputs(inputs)
    nc = build_program()
    res = run_bass_kernel_spmd(nc, maps, core_ids=list(range(8)))
    outp = np.zeros((4, SEQ, D), np.float32)
    for core in range(8):
        b, half = core // 2, core % 2
        o = np.asarray(res.results[core]["out"], np.float32)
        if half == 0:
            outp[b, :TOWN] = o
        else:
            outp[b, TOWN:] = o[::-1]
    return outp
```

```python
import contextlib
import numpy as np
import concourse.bass as bass
import concourse.mybir as mybir
from concourse.bass_utils import run_bass_kernel_spmd

F32 = mybir.dt.float32
BF16 = mybir.dt.bfloat16
AF = mybir.ActivationFunctionType
ALU = mybir.AluOpType

D = 2048
SEQ = 8192
TOWN = 4096
NCOL = 7200
EPS = 1e-6
DEBUG = {"stop": None, "export": False}


class Tok:
    __slots__ = ("sem", "key", "val")

    def __init__(self, sem, key, val):
        self.sem, self.key, self.val = sem, key, val


class Buf:
    __slots__ = ("w", "r")

    def __init__(self):
        self.w = None
        self.r = {}


class Eng:
    def __init__(self, name, eng, sem):
        self.name, self.eng, self.sem = name, eng, sem
        self.count = 0
        self.seen = {}


class Ctx:
    def __init__(self, nc, n_dma_sems=40):
        self.nc = nc
        self.E = {}
        for name, eng in (("pe", nc.tensor), ("act", nc.scalar), ("dve", nc.vector),
                          ("pool", nc.gpsimd), ("sp", nc.sync)):
            self.E[name] = Eng(name, eng, nc.alloc_semaphore("s_" + name))
        self.dsems = [[nc.alloc_semaphore("d%d" % i), "d%d" % i, 0] for i in range(n_dma_sems)]
        self.dnext = 0

    def _wait_tok(self, e, t):
        if t is None:
            return
        if e.name == "pe" and t.key == "s_pe":
            return
        if e.seen.get(t.key, 0) >= t.val:
            return
        e.eng.wait_ge(t.sem, t.val)
        e.seen[t.key] = t.val

    def _deps(self, e, reads, writes):
        for b in reads:
            self._wait_tok(e, b.w)
        for b in writes:
            self._wait_tok(e, b.w)
            for t in list(b.r.values()):
                self._wait_tok(e, t)

    def _reg(self, tok, reads, writes):
        for b in writes:
            b.w = tok
            b.r = {}
        for b in reads:
            b.r[tok.key] = tok

    def op(self, ename, fn, reads=(), writes=()):
        e = self.E[ename]
        self._deps(e, reads, writes)
        ins = fn(e.eng)
        e.count += 1
        ins.then_inc(e.sem, 1)
        tok = Tok(e.sem, "s_" + ename, e.count)
        self._reg(tok, reads, writes)
        return tok

    def group(self, ename, fns, reads=(), writes=()):
        e = self.E[ename]
        self._deps(e, reads, writes)
        ins = None
        for fn in fns:
            ins = fn(e.eng)
        e.count += 1
        ins.then_inc(e.sem, 1)
        tok = Tok(e.sem, "s_" + ename, e.count)
        self._reg(tok, reads, writes)
        return tok

    def dma(self, qname, out, in_, reads=(), writes=(), **kw):
        if DEBUG.get("nopooldma") and qname == "pool":
            qname = "sp"
        e = self.E[qname]
        self._deps(e, reads, writes)
        d = self.dsems[self.dnext]
        self.dnext = (self.dnext + 1) % len(self.dsems)
        if d[2] > 0 and e.seen.get(d[1], 0) < d[2]:
            e.eng.wait_ge(d[0], d[2])
            e.seen[d[1]] = d[2]
        e.eng.dma_start(out=out, in_=in_, **kw).then_inc(d[0], 16)
        d[2] += 16
        tok = Tok(d[0], d[1], d[2])
        self._reg(tok, reads, writes)
        return tok

    def barrier(self):
        for e in self.E.values():
            for o in self.E.values():
                if o is not e and o.count > 0:
                    self._wait_tok(e, Tok(o.sem, "s_" + o.name, o.count))
            for d in self.dsems:
                if d[2] > 0:
                    self._wait_tok(e, Tok(d[0], d[1], d[2]))

    def final_wait(self, ename="sp"):
        e = self.E[ename]
        for o in self.E.values():
            if o is not e and o.count > 0:
                self._wait_tok(e, Tok(o.sem, "s_" + o.name, o.count))
        for d in self.dsems:
            if d[2] > 0:
                self._wait_tok(e, Tok(d[0], d[1], d[2]))


def build_program():
    nc = bass.Bass("TRN2", target_bir_lowering=False)
    cx = Ctx(nc)

    def din(name, shape, dt=F32):
        return nc.dram_tensor(name, list(shape), dt, kind="ExternalInput").ap()

    scratch_kind = "ExternalOutput" if DEBUG["export"] else "Internal"

    def dscr(name, shape, dt=BF16):
        return nc.dram_tensor(name, list(shape), dt, kind=scratch_kind).ap()

    x = din("x", [SEQ, D])
    c_in = din("c", [D])
    w_cond = din("w_cond", [D, 3 * D])
    b_cond = din("b_cond", [1, 3 * D])
    w_in = din("w_in", [D, NCOL])
    up_f = din("up_f", [16, 512])
    up_b = din("up_b", [16, 512])
    gb_f = din("gb_f", [512])
    gb_b = din("gb_b", [512])
    gnorm = din("gnorm", [1, 1024])
    btab = din("btab", [48, 128, 256])
    w_out = din("w_out", [D, D])
    fgain = din("fgain", [1, D])
    ident_in = din("ident", [128, 128])
    maskf_in = din("maskf", [128, 128])
    maskb_in = din("maskb", [128, 128])
    amask_in = din("amask", [128, 512])
    negm_in = din("negm", [128, 512])
    out = nc.dram_tensor("out", [TOWN, D], F32, kind="ExternalOutput").ap()

    wbf = dscr("wbf", [D, NCOL])
    modrow = dscr("modrow", [1, 3 * D], F32)
    qT = dscr("qT", [1024, TOWN])
    kT = dscr("kT", [1024, TOWN + 1024])
    agT = dscr("agT", [1024, TOWN])
    vtm = dscr("vtm", [1024 + TOWN + 1024, 1024])
    gqT = dscr("gqT", [512, TOWN])
    gkT = dscr("gkT", [512, SEQ])
    gvs = dscr("gvs", [SEQ, 1024])
    ggs = dscr("ggs", [TOWN, 1024])
    lrT = dscr("lrT", [32, SEQ], F32)
    mixT = dscr("mixT", [D, TOWN])
    obs = dscr("obs", [TOWN, 1024], F32)

    ident = nc.alloc_sbuf_tensor("ident_bf", [128, 128], BF16)
    identf = nc.alloc_sbuf_tensor("identf", [128, 128], F32)
    sc1 = nc.alloc_sbuf_tensor("sc1", [128, 16], F32)
    shf = nc.alloc_sbuf_tensor("shf", [128, 16], F32)
    B_ident, B_mod = Buf(), Buf()

    def stop(tag):
        return DEBUG["stop"] == tag

    def finish():
        cx.final_wait("sp")
        return nc

    with contextlib.ExitStack() as es:
        def sb(shape, dt, name):
            return es.enter_context(nc.sbuf_tensor(name, list(shape), dt))

        def ps(shape, dt, name):
            return es.enter_context(nc.psum_tensor(name, list(shape), dt))

        cfm = sb([128, 16], F32, "p0_c")
        scl = sb([128, 16], F32, "p0_sc")
        wc = [sb([128, 4, 2048], F32, "p0_w%d" % i) for i in range(3)]
        mrow = sb([1, 3 * D], F32, "p0_mrow")
        brow = sb([1, 3 * D], F32, "p0_brow")
        pacc = [ps([1, 2048], F32, "p0_pa%d" % i) for i in range(2)]
        B_c, B_scl, B_mrow, B_brow = Buf(), Buf(), Buf(), Buf()
        B_wc = [Buf(), Buf(), Buf()]
        B_pacc = [Buf(), Buf()]

        cx.dma("sp", identf[:], ident_in[:, :], writes=[B_ident])
        cx.op("dve", lambda e: e.tensor_copy(out=ident[:], in_=identf[:]), reads=[B_ident], writes=[B_ident])
        cx.dma("sp", cfm[:], c_in.rearrange("(k p) -> p k", p=128), writes=[B_c], allow_slow_non_contiguous=True)
        cx.dma("sp", brow[:], b_cond[:, :], writes=[B_brow])
        cx.op("act", lambda e: e.activation(out=scl[:], in_=cfm[:], func=AF.Silu), reads=[B_c], writes=[B_scl])
        wcv = w_cond.rearrange("(k p) n -> p k n", p=128)
        it = 0
        for n3 in range(3):
            pa = pacc[n3 % 2]
            Bpa = B_pacc[n3 % 2]
            for k4 in range(4):
                wb = wc[it % 3]
                Bw = B_wc[it % 3]
                cx.dma(("sp", "pool", "act")[it % 3], wb[:], wcv[:, k4 * 4:(k4 + 1) * 4, n3 * 2048:(n3 + 1) * 2048],
                       writes=[Bw])
                fns = []
                for kk in range(4):
                    kc = k4 * 4 + kk
                    for nn in range(4):
                        fns.append(lambda e, kk=kk, nn=nn, kc=kc, wb=wb, pa=pa: e.matmul(
                            pa[:, nn * 512:(nn + 1) * 512], lhsT=scl[:, kc:kc + 1],
                            rhs=wb[:, kk, nn * 512:(nn + 1) * 512], start=(kc == 0), stop=(kc == 15)))
                cx.group("pe", fns, reads=[Bw, B_scl], writes=[Bpa])
                it += 1
            cx.op("dve", lambda e, n3=n3, pa=pa: e.tensor_tensor(out=mrow[:, n3 * 2048:(n3 + 1) * 2048], in0=pa[:, :],
                                                         in1=brow[:, n3 * 2048:(n3 + 1) * 2048], op=ALU.add),
                  reads=[Bpa, B_brow], writes=[B_mrow])
        B_modrow = Buf()
        cx.dma("sp", modrow[:, :], mrow[:], reads=[B_mrow], writes=[B_modrow])
        cx.dma("sp", shf[:], modrow[0, 0:D].rearrange("(k p) -> p k", p=128), reads=[B_modrow], writes=[B_mod],
               allow_slow_non_contiguous=True)
        cx.dma("sp", sc1[:], modrow[0, D:2 * D].rearrange("(k p) -> p k", p=128), reads=[B_modrow], writes=[B_mod],
               allow_slow_non_contiguous=True)
        cx.op("dve", lambda e: e.tensor_scalar(out=sc1[:], in0=sc1[:], scalar1=1.0, scalar2=None, op0=ALU.add),
              reads=[B_mod], writes=[B_mod])
        cx.barrier()
    if stop("p0"):
        return finish()

    OWN, HALO, FAR = 0, 1, 2
    GROUPS = [
        (0, 512, "fm", qT, 0, (OWN,)), (512, 512, "fm", qT, 512, (OWN,)),
        (1024, 512, "fm", kT, 0, (OWN, HALO)), (1536, 512, "fm", kT, 512, (OWN, HALO)),
        (2048, 512, "tm", vtm, 0, (OWN, HALO)), (2560, 512, "tm", vtm, 512, (OWN, HALO)),
        (3072, 512, "fm", agT, 0, (OWN,)), (3584, 512, "fm", agT, 512, (OWN,)),
        (4096, 512, "fm", gqT, 0, (OWN,)),
        (4608, 512, "fm", gkT, 0, (OWN, HALO, FAR)),
        (5120, 512, "tm", gvs, 0, (OWN, HALO, FAR)), (5632, 512, "tm", gvs, 512, (OWN, HALO, FAR)),
        (6144, 512, "tm", ggs, 0, (OWN,)), (6656, 512, "tm", ggs, 512, (OWN,)),
        (7168, 32, "lr", lrT, 0, (OWN, HALO, FAR)),
    ]
    with contextlib.ExitStack() as es:
        def sb(shape, dt, name):
            return es.enter_context(nc.sbuf_tensor(name, list(shape), dt))

        def ps(shape, dt, name):
            return es.enter_context(nc.psum_tensor(name, list(shape), dt))

        xf = [sb([128, D], F32, "a_xf%d" % i) for i in range(2)]
        junk = sb([128, D], BF16, "a_junk")
        xs = [sb([128, 4, D], BF16, "a_xs%d" % i) for i in range(2)]
        hT = sb([128, 16, 1024], BF16, "a_hT")
        wg = [sb([128, 16, 512], BF16, "a_wg%d" % i) for i in range(3)]
        wf32 = [sb([128, 16, 256], F32, "a_wf%d" % i) for i in range(2)]
        B_wf32 = [Buf(), Buf()]
        B_wbfg = [Buf() for _ in range(len(GROUPS))]
        w_inv = w_in.rearrange("(k p) c -> p k c", p=128)
        nwf = [0]
        stg = [sb([128, 4096], BF16, "a_stg%d" % i) for i in range(3)]
        lrstg = sb([32, 1024], F32, "a_lrstg")
        ssq = sb([128, 8], F32, "a_ssq")
        rst = sb([128, 8], F32, "a_rst")
        zero = sb([128, 1024], BF16, "a_zero")
        pp = [ps([128, 512], F32, "a_pp%d" % i) for i in range(4)]
        pt = [ps([128, 512], BF16, "a_pt%d" % i) for i in range(4)]
        B_xf = [Buf(), Buf()]
        B_junk = Buf()
        B_xs = [[Buf() for _ in range(4)] for _ in range(2)]
        B_hT = [[Buf(), Buf()] for _ in range(16)]
        B_wg = [Buf() for _ in range(3)]
        B_stg = [Buf() for _ in range(3)]
        B_lrstg = Buf()
        B_ss = [Buf() for _ in range(8)]
        B_pp = [Buf() for _ in range(4)]
        B_pt = [Buf() for _ in range(4)]
        B_zero = Buf()
        B_scr = Buf()

        cx.op("pool", lambda e: e.memset(zero[:], 0.0), writes=[B_zero])
        for i in range(8):
            cx.dma("pool", vtm[i * 128:(i + 1) * 128, :], zero[:], reads=[B_zero], writes=[B_scr])

        wbv = wbf.rearrange("(k p) c -> p k c", p=128)
        nx = 0
        npp = 0
        nwg = 0
        nstg = 0
        nev = 0
        nxc = [0]

        def xprep_sub(ti, g, s):
            tk = ti * 1024 + g * 512 + s * 128
            xb = xf[nxc[0] % 2]
            Bx = B_xf[nxc[0] % 2]
            col = g * 4 + s
            nxc[0] += 1
            cx.dma("sp", xb[:], x[tk:tk + 128, :], writes=[Bx])
            cx.op("act", lambda e: e.activation(out=junk[:], in_=xb[:], func=AF.Square, accum_out=ssq[:, col:col + 1]),
                  reads=[Bx], writes=[B_junk, B_ss[col]])
            cx.op("act", lambda e: e.activation(out=rst[:, col:col + 1], in_=ssq[:, col:col + 1],
                                                func=AF.Sqrt, scale=1.0 / D, bias=EPS),
                  reads=[B_ss[col]], writes=[B_ss[col]])
            cx.op("dve", lambda e: e.reciprocal(out=rst[:, col:col + 1], in_=rst[:, col:col + 1]),
                  reads=[B_ss[col]], writes=[B_ss[col]])
            cx.op("dve", lambda e: e.tensor_scalar(out=xs[g][:, s, :], in0=xb[:], scalar1=rst[:, col:col + 1],
                                                   scalar2=None, op0=ALU.mult),
                  reads=[Bx, B_ss[col]], writes=[B_xs[g][s]])

        for g in range(2):
            for s in range(4):
                xprep_sub(0, g, s)
        for ti in range(8):
            ttype = OWN if ti < 4 else (HALO if ti == 4 else FAR)
            t0 = ti * 1024
            for g in range(2):
                for kc in range(16):
                    pb = pt[kc % 4]
                    Bp = B_pt[kc % 4]
                    fns = [lambda e, s=s, kc=kc, pb=pb, g=g: e.transpose(out=pb[:, s * 128:(s + 1) * 128],
                                                                        in_=xs[g][:, s, kc * 128:(kc + 1) * 128],
                                                                        identity=ident[:]) for s in range(4)]
                    cx.group("pe", fns, reads=B_xs[g] + [B_ident], writes=[Bp])
                    cx.op("act", lambda e, kc=kc, pb=pb, g=g: e.activation(
                        out=hT[:, kc, g * 512:(g + 1) * 512], in_=pb[:], func=AF.Identity,
                        scale=sc1[:, kc:kc + 1], bias=shf[:, kc:kc + 1]),
                          reads=[Bp, B_mod], writes=[B_hT[kc][g]])
            nxt = [(g, s) for g in range(2) for s in range(4)] if ti + 1 < 8 else []
            ngroups_t = sum(1 for g_ in GROUPS if ttype in g_[5])
            per_group = -(-len(nxt) // ngroups_t) if nxt else 0
            first_group = True
            for (c0, ncl, kind, dest, doff, types) in GROUPS:
                if ttype not in types:
                    continue
                w = wg[nwg % 3]
                Bw = B_wg[nwg % 3]
                nwg += 1
                gi_ = [g_[0] for g_ in GROUPS].index(c0)
                if ti == 0:
                    for hf in range((ncl + 255) // 256):
                        wd = min(256, ncl - hf * 256)
                        fb = wf32[nwf[0] % 2]
                        Bfb = B_wf32[nwf[0] % 2]
                        cx.dma("sp", fb[:, :, 0:wd], w_inv[:, :, c0 + hf * 256:c0 + hf * 256 + wd], writes=[Bfb])
                        cx.op("dve", lambda e, fb=fb, wd=wd, hf=hf, w=w: e.tensor_copy(
                            out=w[:, :, hf * 256:hf * 256 + wd], in_=fb[:, :, 0:wd]), reads=[Bfb], writes=[Bw])
                        nwf[0] += 1
                    cx.dma("pool", wbv[:, :, c0:c0 + ncl], w[:, :, 0:ncl], reads=[Bw], writes=[B_wbfg[gi_]])
                else:
                    cx.dma("sp", w[:, :, 0:ncl], wbv[:, :, c0:c0 + ncl], reads=[B_wbfg[gi_]], writes=[Bw])
                if kind == "fm":
                    st = stg[nstg % 3]
                    Bs = B_stg[nstg % 3]
                    nstg += 1
                    for c in range(4):
                        for th in range(2):
                            pb = pp[npp % 4]
                            Bp = B_pp[npp % 4]
                            npp += 1
                            fns = [lambda e, kc=kc, c=c, th=th, pb=pb, w=w: e.matmul(
                                pb[:, :], lhsT=w[:, kc, c * 128:(c + 1) * 128], rhs=hT[:, kc, th * 512:(th + 1) * 512],
                                start=(kc == 0), stop=(kc == 15)) for kc in range(16)]
                            cx.group("pe", fns, reads=[Bw] + [B_hT[kc][th] for kc in range(16)], writes=[Bp])
                            o = st[:, c * 1024 + th * 512: c * 1024 + (th + 1) * 512]
                            if nev % 2 == 0:
                                cx.op("act", lambda e, o=o, pb=pb: e.copy(out=o, in_=pb[:, :]), reads=[Bp], writes=[Bs])
                            else:
                                cx.op("dve", lambda e, o=o, pb=pb: e.tensor_copy(out=o, in_=pb[:, :]), reads=[Bp], writes=[Bs])
                            nev += 1
                    cx.dma("pool", dest[doff:doff + 512, t0:t0 + 1024].rearrange("(c p) t -> p c t", p=128),
                           st[:, :].rearrange("p (c t) -> p c t", c=4), reads=[Bs], writes=[B_scr])
                elif kind == "tm":
                    st = stg[nstg % 3]
                    Bs = B_stg[nstg % 3]
                    nstg += 1
                    for s in range(8):
                        pb = pp[npp % 4]
                        Bp = B_pp[npp % 4]
                        npp += 1
                        fns = [lambda e, kc=kc, s=s, pb=pb, w=w: e.matmul(
                            pb[:, :], lhsT=hT[:, kc, s * 128:(s + 1) * 128], rhs=w[:, kc, 0:512],
                            start=(kc == 0), stop=(kc == 15)) for kc in range(16)]
                        cx.group("pe", fns, reads=[Bw] + [B_hT[kc][s // 4] for kc in range(16)], writes=[Bp])
                        o = st[:, s * 512:(s + 1) * 512]
                        if nev % 2 == 0:
                            cx.op("act", lambda e, o=o, pb=pb: e.copy(out=o, in_=pb[:, :]), reads=[Bp], writes=[Bs])
                        else:
                            cx.op("dve", lambda e, o=o, pb=pb: e.tensor_copy(out=o, in_=pb[:, :]), reads=[Bp], writes=[Bs])
                        nev += 1
                    roff = 1024 if dest is vtm else 0
                    cx.dma("pool", dest[roff + t0:roff + t0 + 1024, doff:doff + 512].rearrange("(s p) c -> p s c", p=128),
                           st[:, :].rearrange("p (s c) -> p s c", s=8), reads=[Bs], writes=[B_scr])
                else:
                    for th in range(2):
                        pb = pp[npp % 4]
                        Bp = B_pp[npp % 4]
                        npp += 1
                        fns = [lambda e, kc=kc, th=th, pb=pb, w=w: e.matmul(
                            pb[0:32, :], lhsT=w[:, kc, 0:32], rhs=hT[:, kc, th * 512:(th + 1) * 512],
                            start=(kc == 0), stop=(kc == 15)) for kc in range(16)]
                        cx.group("pe", fns, reads=[Bw] + [B_hT[kc][th] for kc in range(16)], writes=[Bp])
                        cx.op("dve", lambda e, th=th, pb=pb: e.tensor_copy(out=lrstg[:, th * 512:(th + 1) * 512],
                                                                           in_=pb[0:32, :]), reads=[Bp], writes=[B_lrstg])
                    cx.dma("pool", lrT[:, t0:t0 + 1024], lrstg[:], reads=[B_lrstg], writes=[B_scr])
                for _ in range(per_group):
                    if nxt:
                        xprep_sub(ti + 1, *nxt.pop(0))
            while nxt:
                xprep_sub(ti + 1, *nxt.pop(0))
        cx.barrier()
    if stop("pA"):
        return finish()

    with contextlib.ExitStack() as es:
        def sb(shape, dt, name):
            return es.enter_context(nc.sbuf_tensor(name, list(shape), dt))

        def ps(shape, dt, name):
            return es.enter_context(nc.psum_tensor(name, list(shape), dt))

        qn = sb([128, TOWN], BF16, "b_qn")
        kn = sb([128, 64 + TOWN + 1024], BF16, "b_kn")
        qp = sb([128, TOWN], BF16, "b_qp")
        kp = sb([128, 6144], BF16, "b_kp")
        vraw = [sb([128, 48, 128], BF16, "b_vraw%d" % i) for i in range(2)]
        vaug = [sb([128, 48, 128], BF16, "b_vaug%d" % i) for i in range(2)]
        badd = [[sb([128, 1024], F32, "b_badd%d_%d" % (pi, hd)) for hd in range(2)] for pi in range(3)]
        braw = sb([128, 256], F32, "b_braw")
        amask = sb([128, 512], F32, "b_amask")
        negm = sb([128, 512], F32, "b_negm")
        sadd = [sb([128, 512], F32, "b_sadd%d" % i) for i in range(4)]
        pT = [sb([128, 512], BF16, "b_pT%d" % i) for i in range(4)]
        acc = [sb([128, TOWN], F32, "b_acc%d" % i) for i in range(2)]
        densh = sb([64, 2048], F32, "b_densh")
        agt = sb([64, 2048], BF16, "b_ag")
        sil = sb([64, 2048], F32, "b_sil")
        aout = sb([64, 2048], BF16, "b_aout")
        ps_s = [ps([128, 512], F32, "b_pss%d" % i) for i in range(4)]
        ps_o = [ps([128, 256], F32, "b_pso%d" % i) for i in range(4)]
        B_qn, B_kn, B_qp, B_kp = Buf(), Buf(), Buf(), Buf()
        B_vraw = [Buf(), Buf()]
        B_vaug = [Buf(), Buf()]
        ngrp = {"v": 0, "a": 0, "ss": 0, "so": 0}
        B_badd = [[Buf(), Buf()] for _ in range(3)]
        B_braw, B_am = Buf(), Buf()
        B_sadd = [Buf() for _ in range(4)]
        B_pT = [Buf() for _ in range(4)]
        B_acc = [Buf(), Buf()]
        B_densh, B_ag, B_sil, B_aout = Buf(), Buf(), Buf(), Buf()
        B_pss = [Buf() for _ in range(4)]
        B_pso = [Buf() for _ in range(4)]
        B_mix = Buf()

        cx.dma("sp", amask[:], amask_in[:, :], writes=[B_am])
        cx.dma("sp", negm[:], negm_in[:, :], writes=[B_am])
        cx.op("pool", lambda e: e.memset(kn[:, 0:64], 0.0), writes=[B_kn])
        for i in range(2):
            cx.op("pool", lambda e, i=i: e.memset(vaug[i][:, :, 64:128], 1.0), writes=[B_vaug[i]])
        pending_tail = [None]

        def tail(hp, hd):
            frow = (hp * 2 + hd) * 64
            cx.op("act", lambda e: e.activation(out=acc[hd][64:128, :], in_=acc[hd][64:128, :], func=AF.Ln),
                  writes=[B_acc[hd]])
            cx.op("act", lambda e: e.activation(out=acc[hd][64:128, :], in_=acc[hd][64:128, :], func=AF.Exp, scale=-1.0),
                  writes=[B_acc[hd]])
            for hf in range(2):
                cs = slice(hf * 2048, (hf + 1) * 2048)
                cx.dma("sp", densh[:, :], acc[hd][64:128, cs], reads=[B_acc[hd]], writes=[B_densh])
                cx.dma("sp", agt[:, :], agT[frow:frow + 64, cs], writes=[B_ag])
                cx.op("act", lambda e: e.activation(out=sil[:], in_=agt[:], func=AF.Silu), reads=[B_ag], writes=[B_sil])
                cx.op("pool", lambda e, cs=cs: e.tensor_tensor(out=acc[hd][0:64, cs], in0=acc[hd][0:64, cs],
                                                               in1=densh[:, :], op=ALU.mult),
                      reads=[B_densh], writes=[B_acc[hd]])
                cx.op("pool", lambda e, cs=cs: e.tensor_tensor(out=aout[:], in0=acc[hd][0:64, cs], in1=sil[:], op=ALU.mult),
                      reads=[B_sil, B_acc[hd]], writes=[B_aout])
                cx.dma("pool", mixT[frow:frow + 64, cs], aout[:], reads=[B_aout], writes=[B_mix])

        for hp in range(8):
            cx.dma("sp", qn[:], qT[hp * 128:(hp + 1) * 128, :], writes=[B_qn])
            cx.dma("sp", kn[:, 64:64 + TOWN + 1024], kT[hp * 128:(hp + 1) * 128, :], writes=[B_kn])
            for pi in range(3):
                for hd in range(2):
                    cx.dma("sp", braw[:], btab[pi * 16 + hp * 2 + hd, :, :], writes=[B_braw])
                    bt = badd[pi][hd]
                    cx.op("dve", lambda e, bt=bt: e.tensor_tensor(out=bt[:, 0:256], in0=braw[:], in1=amask[:, 0:256], op=ALU.mult),
                          reads=[B_braw, B_am], writes=[B_badd[pi][hd]])
                    cx.op("dve", lambda e, bt=bt: e.tensor_tensor(out=bt[:, 256:512], in0=braw[:], in1=amask[:, 256:512], op=ALU.mult),
                          reads=[B_braw, B_am], writes=[B_badd[pi][hd]])
                    cx.op("dve", lambda e, bt=bt: e.tensor_tensor(out=bt[:, 0:512], in0=bt[:, 0:512], in1=negm[:], op=ALU.add),
                          reads=[B_am], writes=[B_badd[pi][hd]])
                    cx.op("dve", lambda e, bt=bt: e.tensor_copy(out=bt[:, 512:768], in_=bt[:, 256:512]), writes=[B_badd[pi][hd]])
                    cx.op("dve", lambda e, bt=bt: e.tensor_copy(out=bt[:, 768:1024], in_=bt[:, 256:512]), writes=[B_badd[pi][hd]])
            items = []
            pat_hooks = []
            for pi, d in enumerate((1, 4, 16)):
                Mown = TOWN // d
                seg = Mown + 128
                nq = Mown // 128
                nch = nq + 1
                if d == 1:
                    qv, kv = qn, kn
                    Bq, Bk = B_qn, B_kn
                else:
                    qv, kv = qp, kp
                    Bq, Bk = B_qp, B_kp
                vb = ngrp["v"] % 2
                ngrp["v"] += 1

                def pattern_vload(d=d, nch=nch, vb=vb):
                    base = 1024 - 64 * d
                    vsrc = vtm[base:base + nch * 128 * d, hp * 128:(hp + 1) * 128].rearrange("(c p r) f -> r p c f", p=128, r=d)
                    for r in range(d):
                        cx.dma("sp", vraw[vb][:, r * nch:(r + 1) * nch, :], vsrc[r], writes=[B_vraw[vb]])

                def pattern_pre(d=d, Mown=Mown, seg=seg, nch=nch, vb=vb):
                    if d > 1:
                        cx.op("act", lambda e: e.copy(out=qp[:, :].rearrange("p (r m) -> p r m", r=d),
                                                      in_=qn[:, :].rearrange("p (m r) -> p r m", r=d)),
                              reads=[B_qn], writes=[B_qp])
                        kpv = kp[:, 0:d * seg].rearrange("p (r s) -> p r s", r=d)
                        cx.op("pool", lambda e: e.memset(kpv[:, :, 0:64], 0.0), writes=[B_kp])
                        cx.op("act", lambda e: e.copy(
                            out=kpv[:, :, 64:64 + Mown + 64],
                            in_=kn[:, 64:64 + (Mown + 64) * d].rearrange("p (m r) -> p r m", r=d)),
                              reads=[B_kn], writes=[B_kp])
                pat_hooks.append((pattern_pre, pattern_vload))

                for hd in range(2):
                    lo, hi = hd * 64, (hd + 1) * 64
                    ab = ngrp["a"] % 2
                    ngrp["a"] += 1

                    def group_pre(lo=lo, hi=hi, d=d, nch=nch, vb=vb, ab=ab):
                        cx.op("act", lambda e: e.copy(out=vaug[ab][:, 0:d * nch, 0:64], in_=vraw[vb][:, 0:d * nch, lo:hi]),
                              reads=[B_vraw[vb]], writes=[B_vaug[ab]])

                    bt = badd[pi][hd]
                    first = True
                    for r in range(d):
                        for i2 in range(nq // 2):
                            pre = []
                            if first:
                                pre.append(group_pre)
                                first = False
                            st = {}

                            def s1(st=st, r=r, i2=i2, lo=lo, hi=hi, qv=qv, kv=kv, Bq=Bq, Bk=Bk, Mown=Mown, seg=seg, bt=bt, pi=pi, hd=hd):
                                k_ = ngrp["ss"] % 4
                                ngrp["ss"] += 1
                                st["k"] = k_
                                pss, sa, pt_ = ps_s[k_], sadd[k_], pT[k_]
                                fns = []
                                q0 = r * Mown + 256 * i2
                                kb0 = r * seg + 256 * i2
                                for j, (qa, qb, oa, ob) in enumerate(((0, 128, 0, 128), (0, 256, 128, 384), (128, 256, 384, 512))):
                                    fns.append(lambda e, j=j, qa=qa, qb=qb, oa=oa, ob=ob: e.matmul(
                                        pss[:, oa:ob], lhsT=kv[lo:hi, kb0 + 128 * j:kb0 + 128 * j + 128],
                                        rhs=qv[lo:hi, q0 + qa:q0 + qb], start=True, stop=True))
                                cx.group("pe", fns, reads=[Bq, Bk], writes=[B_pss[k_]])
                                boff = 0 if i2 == 0 else 512
                                cx.op("dve", lambda e: e.scalar_tensor_tensor(
                                    out=sa[:], in0=pss[:, :], scalar=0.125, in1=bt[:, boff:boff + 512], op0=ALU.mult, op1=ALU.add),
                                      reads=[B_pss[k_], B_badd[pi][hd]], writes=[B_sadd[k_]])
                                cx.op("act", lambda e: e.activation(out=pt_[:], in_=sa[:], func=AF.Exp),
                                      reads=[B_sadd[k_]], writes=[B_pT[k_]])

                            def s2(st=st, r=r, i2=i2, d=d, nch=nch, ab=ab, pi=pi, hd=hd):
                                k_ = st["k"]
                                pt_ = pT[k_]
                                o_ = ngrp["so"] % 4
                                ngrp["so"] += 1
                                pso = ps_o[o_]
                                fns = []
                                ch0 = r * nch + 2 * i2
                                for j, (pa, pb, oa, ob) in enumerate(((0, 128, 0, 128), (128, 384, 0, 256), (384, 512, 128, 256))):
                                    fns.append(lambda e, j=j, pa=pa, pb=pb, oa=oa, ob=ob: e.matmul(
                                        pso[:, oa:ob], lhsT=vaug[ab][:, ch0 + j, :], rhs=pt_[:, pa:pb],
                                        start=(j == 0), stop=(j == 2), skip_group_check=True))
                                cx.group("pe", fns, reads=[B_vaug[ab], B_pT[k_]], writes=[B_pso[o_]])
                                a0 = r + 256 * i2 * d
                                av = acc[hd][:, a0:a0 + 255 * d + 1:d] if d > 1 else acc[hd][:, a0:a0 + 256]
                                if pi == 0:
                                    cx.op("act", lambda e: e.copy(out=av, in_=pso[:, :]), reads=[B_pso[o_]], writes=[B_acc[hd]])
                                else:
                                    cx.op("dve", lambda e: e.tensor_tensor(out=av, in0=pso[:, :], in1=av, op=ALU.add),
                                          reads=[B_pso[o_]], writes=[B_acc[hd]])

                            items.append((pre, s1, s2))
            items[0][0][0:0] = [pat_hooks[0][0], pat_hooks[0][1]]
            items[2][0][0:0] = [pat_hooks[1][0], pat_hooks[1][1]]
            items[20][0][0:0] = [pat_hooks[2][1]]
            items[64][0][0:0] = [pat_hooks[2][0]]
            if pending_tail[0] is not None:
                items[4][0].append(pending_tail[0])
                pending_tail[0] = None
            items[84][0].append(lambda hp=hp: tail(hp, 0))
            pending_tail[0] = (lambda hp=hp: tail(hp, 1))
            LOOK = 3
            for j in range(len(items) + LOOK):
                if j < len(items):
                    for f in items[j][0]:
                        f()
                    items[j][1]()
                if j - LOOK >= 0:
                    items[j - LOOK][2]()
        if pending_tail[0] is not None:
            pending_tail[0]()
        cx.barrier()
    if stop("pB"):
        return finish()

    LN_QS = -2.4260151319598084
    with contextlib.ExitStack() as es:
        def sb(shape, dt, name):
            return es.enter_context(nc.sbuf_tensor(name, list(shape), dt))

        def ps(shape, dt, name):
            return es.enter_context(nc.psum_tensor(name, list(shape), dt))

        ps_o = ps([128, 1024], F32, "c_pso")
        ps_z = ps([128, 512], F32, "c_psz")
        ps_gt = ps([128, 1024], BF16, "c_psgt")
        ps_Pb = [ps([128, 512], F32, "c_psP%d" % i) for i in range(2)]
        ps_P = [ps_Pb[0][:, 0:256], ps_Pb[0][:, 256:512], ps_Pb[1][:, 0:256], ps_Pb[1][:, 256:512]]
        ps_ab = ps([128, 512], F32, "c_psa")
        ps_a = [ps_ab[:, i * 128:(i + 1) * 128] for i in range(4)]
        if DEBUG.get("skipo"):
            ps_a = [ps_o[:, i * 128:(i + 1) * 128] for i in range(4)]
        ps_tb = ps([128, 512], BF16, "c_pst")
        ps_t = [ps_tb[:, i * 128:(i + 1) * 128] for i in range(4)]
        upw = [sb([96, 512], F32, "c_up%d" % i) for i in range(2)]
        upb3 = [sb([96, 512], BF16, "c_upb%d" % i) for i in range(2)]
        uptmp = sb([96, 512], BF16, "c_uptmp")
        lr3 = [sb([96, 512], BF16, "c_lr3_%d" % i) for i in range(2)]
        lrtmp = sb([96, 512], BF16, "c_lrtmp")
        ngb = sb([128, 8], F32, "c_ngb")
        gmask = [sb([128, 128], F32, "c_mask%d" % i) for i in range(2)]
        gainb = sb([128, 1024], F32, "c_gainb")
        lrb = [sb([96, 512], F32, "c_lrb%d" % i) for i in range(2)]
        kblk = [sb([128, 4, 512], BF16, "c_kblk%d" % i) for i in range(2)]
        qblk = [sb([128, 4, 512], BF16, "c_qblk%d" % i) for i in range(2)]
        gvblk = [sb([128, 4, 1024], BF16, "c_gvblk%d" % i) for i in range(2)]
        ggblk = [sb([128, 4, 1024], BF16, "c_ggblk%d" % i) for i in range(2)]
        e1 = [sb([128, 512], F32, "c_e1_%d" % i) for i in range(2)]
        nlg = sb([128, 4, 512], F32, "c_nlg")
        cb = sb([128, 4, 512], F32, "c_cb")
        onesf = sb([128, 128], F32, "c_ones")
        ekb = sb([128, 4, 512], F32, "c_ekb")
        eb = sb([128, 4, 512], F32, "c_eb")
        ebl = [sb([128, 4, 4], F32, "c_ebl%d" % i) for i in range(2)]
        kbT = [sb([128, 4, 512], BF16, "c_kbT%d" % i) for i in range(2)]
        qfT = [sb([128, 4, 512], BF16, "c_qfT%d" % i) for i in range(2)]
        kbtm = [sb([128, 128], BF16, "c_kbtm%d" % i) for i in range(4)]
        attm = [sb([128, 128], BF16, "c_attm%d" % i) for i in range(4)]
        U = [sb([128, 256], F32, "c_U%d" % i) for i in range(4)]
        Sbf = [[sb([128, 256], BF16, "c_Sbf%d_%d" % (j, i)) for i in range(4)] for j in range(2)]
        lnqs = sb([128, 1], F32, "c_lnqs")
        obuf = [sb([128, 1024], F32, "c_obuf%d" % i) for i in range(2)]
        otot = sb([128, 1024], F32, "c_otot")
        junkc = sb([128, 256], BF16, "c_junk")
        ss4 = sb([128, 4], F32, "c_ss4")
        silg = sb([128, 1024], F32, "c_silg")
        gout = sb([128, 1024], BF16, "c_gout")
        gstg = [sb([128, 8, 512], BF16, "c_gstg%d" % i) for i in range(2)]
        B_const = Buf()
        B_lrb = [Buf(), Buf()]
        B_kblk = [Buf(), Buf()]
        B_qblk = [Buf(), Buf()]
        B_gvblk = [Buf(), Buf()]
        B_ggblk = [Buf(), Buf()]
        B_e1 = [Buf(), Buf()]
        B_psz = Buf()
        B_nlg = [Buf() for _ in range(4)]
        B_cb = [Buf() for _ in range(4)]
        B_ekb = [Buf() for _ in range(4)]
        B_eb = [Buf() for _ in range(4)]
        B_ebl = [[Buf() for _ in range(4)] for _ in range(2)]
        B_kbT = [[Buf() for _ in range(4)] for _ in range(2)]
        B_qfT = [[Buf() for _ in range(4)] for _ in range(2)]
        B_kbtm = [Buf() for _ in range(4)]
        B_attm = [Buf() for _ in range(4)]
        B_U = [Buf() for _ in range(4)]
        B_Sbf = [[Buf() for _ in range(4)] for _ in range(2)]
        B_psP = [Buf(), Buf()]
        B_psa = Buf()
        B_pst = Buf()
        B_pso = [Buf(), Buf()]
        B_psgt = Buf()
        B_obuf = [Buf(), Buf()]
        B_otot, B_ss4, B_silg, B_gout, B_junkc = Buf(), Buf(), Buf(), Buf(), Buf()
        B_gstg = [Buf(), Buf()]
        B_obs = Buf()
        B_mix2 = Buf()

        for i, upsrc in enumerate((up_f, up_b)):
            cx.op("pool", lambda e, i=i: e.memset(upb3[i][:], 0.0), writes=[B_const])
            cx.op("pool", lambda e, i=i: e.memset(lr3[i][:], 0.0), writes=[B_lrb[i]])
            for g in range(3):
                cx.dma("sp", upw[i][g * 32:g * 32 + 16, :], upsrc[:, :], writes=[B_const])
            cx.op("dve", lambda e, i=i: e.tensor_copy(out=upb3[i][0:16, :], in_=upw[i][0:16, :]), reads=[B_const], writes=[B_const])
            cx.op("dve", lambda e, i=i: e.tensor_copy(out=upb3[i][32:48, :], in_=upw[i][32:48, :]), reads=[B_const], writes=[B_const])
            cx.op("dve", lambda e, i=i: e.tensor_copy(out=uptmp[64:80, :], in_=upw[i][64:80, :]), reads=[B_const], writes=[B_const])
            cx.op("dve", lambda e, i=i: e.tensor_tensor(out=upb3[i][64:80, :], in0=upw[i][64:80, :], in1=uptmp[64:80, :], op=ALU.subtract),
                  reads=[B_const], writes=[B_const])
        cx.dma("sp", ngb[:, 0:4], gb_f.rearrange("(h p) -> p h", p=128), writes=[B_const], allow_slow_non_contiguous=True)
        cx.dma("sp", ngb[:, 4:8], gb_b.rearrange("(h p) -> p h", p=128), writes=[B_const], allow_slow_non_contiguous=True)
        cx.dma("sp", gmask[0][:], maskf_in[:, :], writes=[B_const])
        cx.dma("sp", gmask[1][:], maskb_in[:, :], writes=[B_const])
        cx.dma("sp", gainb[:], gnorm.partition_broadcast(128), writes=[B_const])
        cx.op("pool", lambda e: e.tensor_scalar(out=ngb[:], in0=ngb[:], scalar1=-1.0, scalar2=None, op0=ALU.mult),
              reads=[B_const], writes=[B_const])
        cx.op("pool", lambda e: e.memset(onesf[:], 1.0), writes=[B_const])
        cx.op("pool", lambda e: e.memset(lnqs[:], LN_QS), writes=[B_const])

        cnt = {"ob": 0}

        def gla_pass(dirn):
            final = (dirn == 0)
            for h in range(4):
                cx.op("pool", lambda e, h=h: e.memset(U[h][:], 0.0), writes=[B_U[h]])
                cx.op("pool", lambda e, h=h: e.memset(Sbf[0][h][:], 0.0), writes=[B_Sbf[0][h]])
            blocks = list(range(8)) if dirn == 0 else list(range(15, -1, -1))
            chunks = [0, 1, 2, 3] if dirn == 0 else [3, 2, 1, 0]
            pos = 127 if dirn == 0 else 0

            def prepA(blk):
                own = blk < 8
                t0 = blk * 512
                bi = blk % 2
                for g in range(3):
                    cx.dma("sp", lrb[bi][g * 32:g * 32 + 16, :], lrT[dirn * 16:(dirn + 1) * 16, t0:t0 + 512], writes=[B_lrb[bi]])
                cx.op("dve", lambda e: e.tensor_copy(out=lr3[bi][0:16, :], in_=lrb[bi][0:16, :]), reads=[B_lrb[bi]], writes=[B_lrb[bi]])
                cx.op("dve", lambda e: e.tensor_copy(out=lr3[bi][64:80, :], in_=lrb[bi][64:80, :]), reads=[B_lrb[bi]], writes=[B_lrb[bi]])
                cx.op("dve", lambda e: e.tensor_copy(out=lrtmp[32:48, :], in_=lrb[bi][32:48, :]), reads=[B_lrb[bi]], writes=[B_lrb[bi]])
                cx.op("dve", lambda e: e.tensor_tensor(out=lr3[bi][32:48, :], in0=lrb[bi][32:48, :], in1=lrtmp[32:48, :], op=ALU.subtract),
                      reads=[B_lrb[bi]], writes=[B_lrb[bi]])
                cx.dma("sp", kblk[bi][:], gkT[:, t0:t0 + 512].rearrange("(h p) t -> p h t", p=128), writes=[B_kblk[bi]])
                cx.dma("sp", gvblk[bi][:], gvs[t0:t0 + 512, :].rearrange("(c p) f -> p c f", p=128), writes=[B_gvblk[bi]])
                if own:
                    cx.dma("sp", qblk[bi][:], gqT[:, t0:t0 + 512].rearrange("(h p) t -> p h t", p=128), writes=[B_qblk[bi]])
                if own and final:
                    cx.dma("sp", ggblk[bi][:], ggs[t0:t0 + 512, :].rearrange("(c p) f -> p c f", p=128), writes=[B_ggblk[bi]])
                for h in range(4):
                    cx.group("pe", [lambda e, h=h: e.matmul(ps_z[:, :], lhsT=upb3[dirn][:, h * 128:(h + 1) * 128],
                                                           rhs=lr3[bi][:, :], start=True, stop=True)],
                             reads=[B_const, B_lrb[bi]], writes=[B_psz])
                    zi = h % 2
                    cx.op("act", lambda e, h=h, zi=zi: e.activation(out=e1[zi][:], in_=ps_z[:, :], func=AF.Exp, scale=-1.0,
                                                                   bias=ngb[:, dirn * 4 + h:dirn * 4 + h + 1]),
                          reads=[B_psz, B_const], writes=[B_e1[zi]])
                    cx.op("act", lambda e, h=h, zi=zi: e.activation(out=nlg[:, h, :], in_=e1[zi][:], func=AF.Ln, bias=1.0),
                          reads=[B_e1[zi]], writes=[B_nlg[h]])
                    for c in range(4):
                        o_ = cb[:, h, c * 128:(c + 1) * 128]
                        i_ = nlg[:, h, c * 128:(c + 1) * 128]
                        on_ = onesf[:, :]
                        if dirn == 1:
                            o_, i_, on_ = o_[:, ::-1], i_[:, ::-1], on_[:, ::-1]
                        cx.op("dve", lambda e, o_=o_, i_=i_, on_=on_: e.tensor_tensor_scan(
                            out=o_, data0=on_, data1=i_, initial=0.0, op0=ALU.mult, op1=ALU.add),
                              reads=[B_nlg[h], B_const], writes=[B_cb[h]])
                    cx.op("act", lambda e, h=h: e.activation(out=ebl[bi][:, h, :], in_=cb[:, h, pos:512:128], func=AF.Exp,
                                                             scale=-1.0 / 16.0),
                          reads=[B_cb[h]], writes=[B_ebl[bi][h]])
                    cx.op("act", lambda e, h=h: e.activation(out=ekb[:, h, :], in_=cb[:, h, :], func=AF.Exp, scale=1.0 / 16.0),
                          reads=[B_cb[h]], writes=[B_ekb[h]])
                    cx.op("pool", lambda e, h=h: e.tensor_tensor(out=kbT[bi][:, h, :], in0=kblk[bi][:, h, :], in1=ekb[:, h, :],
                                                                 op=ALU.mult),
                          reads=[B_kblk[bi], B_ekb[h]], writes=[B_kbT[bi][h]])
                    if own:
                        cx.op("act", lambda e, h=h: e.activation(out=eb[:, h, :], in_=cb[:, h, :], func=AF.Exp, scale=-1.0 / 16.0,
                                                                 bias=lnqs[:, 0:1]),
                              reads=[B_cb[h], B_const], writes=[B_eb[h]])
                        cx.op("pool", lambda e, h=h: e.tensor_tensor(out=qfT[bi][:, h, :], in0=qblk[bi][:, h, :], in1=eb[:, h, :],
                                                                     op=ALU.mult),
                              reads=[B_qblk[bi], B_eb[h]], writes=[B_qfT[bi][h]])

            prepA(blocks[0])
            eprev = [onesf[:, 0:1]] * 4
            Beprev = [B_const] * 4
            par = 0
            for bidx, blk in enumerate(blocks):
                own = blk < 8 and not DEBUG.get("noown")
                t0 = blk * 512
                bi = blk % 2
                for cidx, c in enumerate(chunks):
                    csl = slice(c * 128, (c + 1) * 128)
                    if own:
                        cx.group("pe", [lambda e, h=h: e.matmul(ps_a[h], lhsT=kbT[bi][:, h, csl], rhs=qfT[bi][:, h, csl],
                                                               start=True, stop=True) for h in range(4)],
                                 reads=B_kbT[bi] + B_qfT[bi], writes=[B_psa])
                    cx.group("pe", [lambda e, h=h: e.transpose(out=ps_t[h], in_=kbT[bi][:, h, csl], identity=ident[:])
                                    for h in range(4)], reads=B_kbT[bi] + [B_ident], writes=[B_pst])
                    for h in range(4):
                        cx.op("act", lambda e, h=h: e.copy(out=kbtm[h][:], in_=ps_t[h]), reads=[B_pst], writes=[B_kbtm[h]])
                    if own:
                        for h in range(4):
                            cx.op("dve", lambda e, h=h: e.tensor_tensor(out=attm[h][:], in0=ps_a[h], in1=gmask[dirn][:], op=ALU.mult),
                                  reads=[B_psa, B_const], writes=[B_attm[h]])
                    for hh in range(2):
                        cx.group("pe", [lambda e, h=h: e.matmul(ps_P[h], lhsT=kbtm[h][:, :],
                                                               rhs=gvblk[bi][:, c, h * 256:(h + 1) * 256], start=True, stop=True)
                                        for h in (2 * hh, 2 * hh + 1)],
                                 reads=[B_kbtm[2 * hh], B_kbtm[2 * hh + 1], B_gvblk[bi]], writes=[B_psP[hh]])
                    if own:
                        for hh in range(2):
                            fns = []
                            for h in (2 * hh, 2 * hh + 1):
                                vsl = slice(h * 256, (h + 1) * 256)
                                fns.append(lambda e, h=h, vsl=vsl: e.matmul(ps_o[:, vsl], lhsT=attm[h][:, :], rhs=gvblk[bi][:, c, vsl],
                                                                            start=True, stop=False))
                                fns.append(lambda e, h=h, vsl=vsl: e.matmul(ps_o[:, vsl], lhsT=qfT[bi][:, h, csl], rhs=Sbf[par][h][:, :],
                                                                            start=False, stop=True))
                            cx.group("pe", fns, reads=[B_attm[2 * hh], B_attm[2 * hh + 1], B_gvblk[bi], B_qfT[bi][2 * hh],
                                                       B_qfT[bi][2 * hh + 1], B_Sbf[par][2 * hh], B_Sbf[par][2 * hh + 1]],
                                     writes=[B_pso[hh]])
                    for h in range(4):
                        cx.op("dve", lambda e, h=h, ep=eprev[h]: e.scalar_tensor_tensor(
                            out=U[h][:], in0=U[h][:], scalar=ep, in1=ps_P[h], op0=ALU.mult, op1=ALU.add),
                              reads=[B_psP[h // 2], Beprev[h]], writes=[B_U[h]])
                        eprev[h] = ebl[bi][:, h, c:c + 1]
                        Beprev[h] = B_ebl[bi][h]
                    for h in range(4):
                        cx.op("act", lambda e, h=h: e.activation(out=Sbf[1 - par][h][:], in_=U[h][:], func=AF.Copy,
                                                                 scale=ebl[bi][:, h, c:c + 1]),
                              reads=[B_U[h], B_ebl[bi][h]], writes=[B_Sbf[1 - par][h]])
                    par = 1 - par
                    if cidx == 0 and bidx + 1 < len(blocks):
                        prepA(blocks[bidx + 1])
                    if not own or DEBUG.get("skipfin"):
                        continue
                    tk = t0 + c * 128
                    oi = cnt["ob"] % 2
                    cnt["ob"] += 1
                    if not final:
                        cx.op("act", lambda e, oi=oi: e.copy(out=obuf[oi][:], in_=ps_o[:, :]), reads=B_pso, writes=[B_obuf[oi]])
                        cx.dma("pool", obs[tk:tk + 128, :], obuf[oi][:], reads=[B_obuf[oi]], writes=[B_obs])
                        continue
                    cx.dma("sp", obuf[oi][:], obs[tk:tk + 128, :], writes=[B_obuf[oi]])
                    cx.op("dve", lambda e, oi=oi: e.tensor_tensor(out=otot[:], in0=ps_o[:, :], in1=obuf[oi][:], op=ALU.add),
                          reads=B_pso + [B_obuf[oi]], writes=[B_otot])
                    for h in range(4):
                        vsl = slice(h * 256, (h + 1) * 256)
                        cx.op("act", lambda e, h=h, vsl=vsl: e.activation(out=junkc[:], in_=otot[:, vsl], func=AF.Square,
                                                                         accum_out=ss4[:, h:h + 1]),
                              reads=[B_otot], writes=[B_junkc, B_ss4])
                    cx.op("act", lambda e: e.activation(out=ss4[:], in_=ss4[:], func=AF.Sqrt, scale=1.0 / 256.0, bias=EPS),
                          reads=[B_ss4], writes=[B_ss4])
                    cx.op("dve", lambda e: e.reciprocal(out=ss4[:], in_=ss4[:]), reads=[B_ss4], writes=[B_ss4])
                    cx.op("act", lambda e, c=c: e.activation(out=silg[:], in_=ggblk[bi][:, c, :], func=AF.Silu),
                          reads=[B_ggblk[bi]], writes=[B_silg])
                    cx.op("pool", lambda e: e.tensor_tensor(out=silg[:], in0=silg[:], in1=gainb[:], op=ALU.mult),
                          reads=[B_const], writes=[B_silg])
                    for h in range(4):
                        vsl = slice(h * 256, (h + 1) * 256)
                        cx.op("dve", lambda e, h=h, vsl=vsl: e.scalar_tensor_tensor(
                            out=gout[:, vsl], in0=otot[:, vsl], scalar=ss4[:, h:h + 1], in1=silg[:, vsl], op0=ALU.mult, op1=ALU.mult),
                              reads=[B_otot, B_ss4, B_silg], writes=[B_gout])
                    fns = [lambda e, f=f: e.transpose(out=ps_gt[:, f * 128:(f + 1) * 128], in_=gout[:, f * 128:(f + 1) * 128],
                                                      identity=ident[:]) for f in range(8)]
                    cx.group("pe", fns, reads=[B_gout, B_ident], writes=[B_psgt])
                    gi = (blk % 2)
                    cx.op("act", lambda e, gi=gi, c=c: e.copy(out=gstg[gi][:, :, c * 128:(c + 1) * 128],
                                                              in_=ps_gt[:, :].rearrange("p (f t) -> p f t", f=8)),
                          reads=[B_psgt], writes=[B_gstg[gi]])
                if own and final:
                    gi = (blk % 2)
                    cx.dma("pool", mixT[1024:2048, t0:t0 + 512].rearrange("(f p) t -> p f t", p=128), gstg[gi][:],
                           reads=[B_gstg[gi]], writes=[B_mix2])

        gla_pass(1)
        cx.barrier()
        if not stop("pC1"):
            gla_pass(0)
        cx.barrier()
    if stop("pC"):
        return finish()

    with contextlib.ExitStack() as es:
        def sb(shape, dt, name):
            return es.enter_context(nc.sbuf_tensor(name, list(shape), dt))

        def ps(shape, dt, name):
            return es.enter_context(nc.psum_tensor(name, list(shape), dt))

        wog = sb([128, 16, D], BF16, "d_wog")
        gateb = sb([128, D], F32, "d_gateb")
        fgb = sb([128, D], F32, "d_fgb")
        mixb = [sb([128, 16, 512], BF16, "d_mix%d" % i) for i in range(2)]
        xt = [sb([128, D], F32, "d_xt%d" % i) for i in range(3)]
        ot = [sb([128, D], F32, "d_ot%d" % i) for i in range(2)]
        junkd = sb([128, D], BF16, "d_junk")
        ssd = sb([128, 2], F32, "d_ss")
        ps_y = [ps([128, D], F32, "d_psy%d" % i) for i in range(2)]
        B_wog = [Buf() for _ in range(16)]
        B_gb, B_fg = Buf(), Buf()
        B_mixb = [Buf(), Buf()]
        B_xt = [Buf() for _ in range(3)]
        B_ot = [Buf(), Buf()]
        B_junkd = Buf()
        B_ssd = [Buf(), Buf()]
        B_psy = [Buf(), Buf()]
        B_out = Buf()

        cx.dma("sp", gateb[:], modrow[0:1, 2 * D:3 * D].partition_broadcast(128), writes=[B_gb])
        cx.dma("sp", fgb[:], fgain.partition_broadcast(128), writes=[B_fg])
        nxt = 0
        for kc in range(16):
            xi = nxt % 3
            nxt += 1
            cx.dma("sp" if kc % 2 == 0 else "act", xt[xi][:], w_out[kc * 128:(kc + 1) * 128, :], writes=[B_xt[xi]])
            cx.op("pool" if kc % 3 == 2 else "dve", lambda e, kc=kc, xi=xi: e.tensor_tensor(
                out=wog[:, kc, :], in0=xt[xi][:], in1=gateb[:], op=ALU.mult), reads=[B_xt[xi], B_gb], writes=[B_wog[kc]])
        mixv = mixT.rearrange("(k p) t -> p k t", p=128)
        for blk in range(8):
            mi = blk % 2
            cx.dma("sp", mixb[mi][:], mixv[:, :, blk * 512:(blk + 1) * 512], writes=[B_mixb[mi]])
            for s4 in range(4):
                it = blk * 4 + s4
                tk = it * 128
                xi = nxt % 3
                nxt += 1
                pi_ = it % 2
                cx.dma("sp", xt[xi][:], x[tk:tk + 128, :], writes=[B_xt[xi]])
                fns = []
                for n in range(4):
                    for kc in range(16):
                        fns.append(lambda e, n=n, kc=kc, mi=mi, s4=s4, pi_=pi_: e.matmul(
                            ps_y[pi_][:, n * 512:(n + 1) * 512], lhsT=mixb[mi][:, kc, s4 * 128:(s4 + 1) * 128],
                            rhs=wog[:, kc, n * 512:(n + 1) * 512], start=(kc == 0), stop=(kc == 15)))
                cx.group("pe", fns, reads=[B_mixb[mi]] + B_wog, writes=[B_psy[pi_]])
                cx.op("dve", lambda e, xi=xi, pi_=pi_: e.tensor_tensor(out=xt[xi][:], in0=ps_y[pi_][:, :], in1=xt[xi][:], op=ALU.add),
                      reads=[B_psy[pi_]], writes=[B_xt[xi]])
                cx.op("act", lambda e, xi=xi, pi_=pi_: e.activation(out=junkd[:], in_=xt[xi][:], func=AF.Square,
                                                                   accum_out=ssd[:, pi_:pi_ + 1]),
                      reads=[B_xt[xi]], writes=[B_junkd, B_ssd[pi_]])
                cx.op("act", lambda e, pi_=pi_: e.activation(out=ssd[:, pi_:pi_ + 1], in_=ssd[:, pi_:pi_ + 1], func=AF.Sqrt,
                                                             scale=1.0 / D, bias=EPS), writes=[B_ssd[pi_]])
                cx.op("dve", lambda e, pi_=pi_: e.reciprocal(out=ssd[:, pi_:pi_ + 1], in_=ssd[:, pi_:pi_ + 1]), writes=[B_ssd[pi_]])
                cx.op("dve", lambda e, xi=xi, pi_=pi_: e.scalar_tensor_tensor(
                    out=ot[pi_][:], in0=xt[xi][:], scalar=ssd[:, pi_:pi_ + 1], in1=fgb[:], op0=ALU.mult, op1=ALU.mult),
                      reads=[B_xt[xi], B_ssd[pi_], B_fg], writes=[B_ot[pi_]])
                cx.dma("pool", out[tk:tk + 128, :], ot[pi_][:], reads=[B_ot[pi_]], writes=[B_out])
    return finish()


def host_inputs(inputs):
    x = np.asarray(inputs["x"], np.float32)
    c = np.asarray(inputs["c"], np.float32)
    w_cond = np.ascontiguousarray(np.asarray(inputs["w_cond"], np.float32)[0])
    b_cond = np.ascontiguousarray(np.asarray(inputs["b_cond"], np.float32)[0])[None, :]
    w_in = np.asarray(inputs["w_in"], np.float32)[0]
    w_in_sw = np.ascontiguousarray(np.concatenate([w_in[:, :7168], w_in[:, 7184:7200], w_in[:, 7168:7184]], axis=1))
    w_in = np.ascontiguousarray(w_in)
    upf = np.ascontiguousarray(np.asarray(inputs["gla_gate_up_fwd"], np.float32)[0])
    upb = np.ascontiguousarray(np.asarray(inputs["gla_gate_up_bwd"], np.float32)[0])
    gbf = np.ascontiguousarray(np.asarray(inputs["gla_gate_bias_fwd"], np.float32)[0])
    gbb = np.ascontiguousarray(np.asarray(inputs["gla_gate_bias_bwd"], np.float32)[0])
    gnorm = np.ascontiguousarray(np.asarray(inputs["gla_norm_gain"], np.float32)[0])[None, :]
    rel_bias = np.asarray(inputs["rel_bias"], np.float32)
    w_out = np.ascontiguousarray(np.asarray(inputs["w_out"], np.float32)[0])
    fgain = np.ascontiguousarray(np.asarray(inputs["final_gain"], np.float32))[None, :]

    def bucket(rel):
        nb = 16
        max_exact = 8
        n = np.abs(rel)
        large = max_exact + (np.log(np.maximum(n, 1) / max_exact) / np.log(1024 / max_exact) * (nb - max_exact)).astype(np.int32)
        large = np.minimum(large, nb - 1)
        return (np.where(rel > 0, nb, 0) + np.where(n < max_exact, n, large)).astype(np.int32)

    kl = np.arange(128)[:, None]
    ql = np.arange(128)[None, :]
    step1 = kl - 64 - ql
    step2 = kl + 64 - ql
    btabs = []
    for flip in (False, True):
        tabs = np.zeros((3, 16, 128, 256), np.float32)
        for pi, dil in enumerate((1, 4, 16)):
            for si, st in enumerate((step1, step2)):
                rel = st * dil
                if flip:
                    rel = -rel
                tabs[pi, :, :, si * 128:(si + 1) * 128] = np.transpose(rel_bias[bucket(rel)], (2, 0, 1))
        btabs.append(np.ascontiguousarray(tabs.reshape(48, 128, 256)))
    amask = np.zeros((128, 512), np.float32)
    amask[:, 0:128] = (np.abs(step1) <= 64) & (kl >= 64)
    amask[:, 128:256] = (np.abs(step2) <= 64)
    amask[:, 256:384] = (np.abs(step1) <= 64)
    amask[:, 384:512] = (np.abs(step2) <= 64)
    negm = ((amask - 1.0) * 30000.0).astype(np.float32)
    ident = np.eye(128, dtype=np.float32)
    maskf = (kl <= ql).astype(np.float32)
    maskb = (kl >= ql).astype(np.float32)

    maps = []
    for core in range(8):
        b, half = core // 2, core % 2
        xl = x[b] if half == 0 else x[b, ::-1]
        m = {
            "x": np.ascontiguousarray(xl), "c": np.ascontiguousarray(c[b]),
            "w_cond": w_cond, "b_cond": b_cond,
            "w_in": w_in if half == 0 else w_in_sw,
            "up_f": upf if half == 0 else upb, "up_b": upb if half == 0 else upf,
            "gb_f": gbf if half == 0 else gbb, "gb_b": gbb if half == 0 else gbf,
            "gnorm": gnorm, "btab": btabs[half], "w_out": w_out, "fgain": fgain,
            "ident": ident, "maskf": maskf, "maskb": maskb, "amask": amask, "negm": negm,
        }
        maps.append(m)
    return maps


def kernel(**inputs):
    maps = host_inputs(inputs)
    nc = build_program()
    res = run_bass_kernel_spmd(nc, maps, core_ids=list(range(8)))
    outp = np.zeros((4, SEQ, D), np.float32)
    for core in range(8):
        b, half = core // 2, core % 2
        o = np.asarray(res.results[core]["out"], np.float32)
        if half == 0:
            outp[b, :TOWN] = o
        else:
            outp[b, TOWN:] = o[::-1]
    return outp
```
